# Optimizing a Trainium2 kernel written in Bass

```python
import math
import jax, jax.numpy as jnp
from jax import lax
import numpy as np

D_MODEL = 2048
BATCH = 1
SEQ = 16384
DEPTH = 2

HEAD_DIM_A = 128
HEADS_PER_GROUP_A = 4
DILATED_GROUPS = ((128, 1), (512, 4), (2048, 16))
N_GROUPS_A = len(DILATED_GROUPS)
N_HEADS_A = N_GROUPS_A * HEADS_PER_GROUP_A
WIDTH_A = N_HEADS_A * HEAD_DIM_A
OUT_WIDTH_A = HEADS_PER_GROUP_A * HEAD_DIM_A
N_HEADS_B = 16
QK_NOPE = 128
QK_ROPE = 64
V_DIM = 128
Q_LORA = 512
KV_LORA = 512
QK_DIM_B = QK_NOPE + QK_ROPE
OUT_WIDTH_B = N_HEADS_B * V_DIM
N_BRANCH = 2
N_IN = 3 * WIDTH_A + Q_LORA + KV_LORA + QK_ROPE + N_BRANCH * D_MODEL
D_FF = 5632
CONV_WIDTH = 3
ROPE_THETA = 10000.0
EPS = 1e-6
Q_BLOCK = 128

kernel_name = "hybrid_dilated_mla_convffn_encoder"


def rms_norm(x, g):
    xf = x.astype(jnp.float32)
    y = xf * lax.rsqrt(jnp.mean(xf * xf, axis=-1, keepdims=True) + EPS)
    return (y * g.astype(jnp.float32)).astype(x.dtype)


def rope(x, positions):
    dim = x.shape[-1]
    inv_freq = 1.0 / (ROPE_THETA ** (jnp.arange(0, dim, 2, dtype=jnp.float32) / dim))
    ang = positions.astype(jnp.float32)[..., None] * inv_freq
    cos = jnp.cos(ang)[:, :, None, :]
    sin = jnp.sin(ang)[:, :, None, :]
    xf = x.astype(jnp.float32)
    x1, x2 = xf[..., : dim // 2], xf[..., dim // 2:]
    out = jnp.concatenate([x1 * cos - x2 * sin, x2 * cos + x1 * sin], axis=-1)
    return out.astype(x.dtype)


def dilated_window_attention(q, k, v, window, dilation):
    B, S, H, hd = q.shape
    d = dilation
    w = window // (2 * d)
    L = S // d
    nb = -(-L // w)
    Lp = nb * w
    G = B * d

    def to_sub(t):
        return t.reshape(B, L, d, H, hd).transpose(0, 2, 1, 3, 4).reshape(G, L, H, hd)

    def windows(t):
        tp = jnp.pad(t, ((0, 0), (w, Lp - L + w), (0, 0), (0, 0))).reshape(G, nb + 2, w, H, hd)
        return jnp.concatenate([tp[:, :-2], tp[:, 1:-1], tp[:, 2:]], axis=2)

    qb = jnp.pad(to_sub(q), ((0, 0), (0, Lp - L), (0, 0), (0, 0))).reshape(G, nb, w, H, hd)
    kw = windows(to_sub(k))
    vw = windows(to_sub(v))

    start = jnp.arange(nb)[:, None] * w
    qi = start + jnp.arange(w)[None, :]
    kj = start - w + jnp.arange(3 * w)[None, :]
    dist = kj[:, None, :] - qi[:, :, None]
    valid = (jnp.abs(dist) <= w) & (kj[:, None, :] >= 0) & (kj[:, None, :] < L)

    s = jnp.einsum('gnqhd,gnkhd->ghnqk', qb, kw, preferred_element_type=jnp.float32) * (hd ** -0.5)
    s = jnp.where(valid, s, -jnp.inf)
    m = jnp.max(s, axis=-1, keepdims=True)
    p = jnp.exp(s - m)
    den = jnp.sum(p, axis=-1, keepdims=True)
    o = jnp.einsum('ghnqk,gnkhd->gnqhd', p.astype(v.dtype), vw, preferred_element_type=jnp.float32)
    o = o / den.transpose(0, 2, 3, 1, 4)
    lse = (m + jnp.log(den))[..., 0].transpose(0, 2, 3, 1)

    o = o.reshape(G, Lp, H, hd)[:, :L].reshape(B, d, L, H, hd).transpose(0, 2, 1, 3, 4).reshape(B, S, H, hd)
    lse = lse.reshape(G, Lp, H)[:, :L].reshape(B, d, L, H).transpose(0, 2, 1, 3).reshape(B, S, H)
    return o, lse


def dense_attention(q, k, v, scale):
    B, S, H, dq = q.shape
    dv = v.shape[-1]
    nq = S // Q_BLOCK
    qb = q.reshape(B, nq, Q_BLOCK, H, dq).transpose(1, 0, 2, 3, 4)

    def block(qi):
        s = jnp.einsum('bqhd,bkhd->bhqk', qi, k, preferred_element_type=jnp.float32) * scale
        p = jax.nn.softmax(s, axis=-1)
        o = jnp.einsum('bhqk,bkhd->bqhd', p.astype(v.dtype), v, preferred_element_type=jnp.float32)
        return o.astype(v.dtype)

    out = lax.map(block, qb)
    return out.transpose(1, 0, 2, 3, 4).reshape(B, S, H, dv)


def centred_depthwise_conv(h, w, b):
    S = h.shape[1]
    pad = CONV_WIDTH // 2
    hp = jnp.pad(h, ((0, 0), (pad, CONV_WIDTH - 1 - pad), (0, 0)))
    out = b
    for t in range(CONV_WIDTH):
        out = out + hp[:, t:t + S] * w[t]
    return out


def setup_inputs(seed: int = 0) -> dict:
    key = jax.random.key(seed)
    ks = jax.random.split(key, 20)

    def nrm(k, shape, scale):
        return jax.random.normal(k, shape, jnp.float32) * scale

    x = nrm(ks[0], (BATCH, SEQ, D_MODEL), 1.0)
    offset = jax.random.randint(ks[1], (BATCH, 1), 0, 1024, dtype=jnp.int32)
    positions = (offset + jnp.arange(SEQ, dtype=jnp.int32)[None, :]).astype(jnp.int32)
    return {
        "x": x,
        "positions": positions,
        "norm_mix": 1.0 + nrm(ks[2], (DEPTH, D_MODEL), 0.02),
        "w_in": nrm(ks[3], (DEPTH, D_MODEL, N_IN), D_MODEL ** -0.5),
        "b_gate": nrm(ks[4], (DEPTH, N_BRANCH, D_MODEL), 0.02),
        "norm_q": 1.0 + nrm(ks[5], (DEPTH, Q_LORA), 0.02),
        "w_uq": nrm(ks[6], (DEPTH, Q_LORA, N_HEADS_B * QK_DIM_B), Q_LORA ** -0.5),
        "norm_kv": 1.0 + nrm(ks[7], (DEPTH, KV_LORA), 0.02),
        "w_ukv": nrm(ks[8], (DEPTH, KV_LORA, N_HEADS_B * (QK_NOPE + V_DIM)), KV_LORA ** -0.5),
        "w_oa": nrm(ks[9], (DEPTH, OUT_WIDTH_A, D_MODEL), OUT_WIDTH_A ** -0.5),
        "w_ob": nrm(ks[10], (DEPTH, OUT_WIDTH_B, D_MODEL), OUT_WIDTH_B ** -0.5),
        "w_out": nrm(ks[11], (DEPTH, D_MODEL, D_MODEL), D_MODEL ** -0.5),
        "norm_ffn": 1.0 + nrm(ks[12], (DEPTH, D_MODEL), 0.02),
        "w_up": nrm(ks[13], (DEPTH, D_MODEL, 2 * D_FF), D_MODEL ** -0.5),
        "conv_w": nrm(ks[14], (DEPTH, CONV_WIDTH, 2 * D_FF), CONV_WIDTH ** -0.5),
        "conv_b": nrm(ks[15], (DEPTH, 2 * D_FF), 0.02),
        "w_down": nrm(ks[16], (DEPTH, D_FF, D_MODEL), D_FF ** -0.5),
        "norm_final": 1.0 + nrm(ks[17], (D_MODEL,), 0.02),
    }


def reference(x, positions, norm_mix, w_in, b_gate, norm_q, w_uq, norm_kv, w_ukv,
              w_oa, w_ob, w_out, norm_ffn, w_up, conv_w, conv_b, w_down, norm_final):
    B, S, D = x.shape
    o_qa, o_ka, o_va = 0, WIDTH_A, 2 * WIDTH_A
    o_ql = 3 * WIDTH_A
    o_kvl = o_ql + Q_LORA
    o_kr = o_kvl + KV_LORA
    o_g = o_kr + QK_ROPE

    for l in range(DEPTH):
        h = rms_norm(x, norm_mix[l])
        p = h @ w_in[l]

        qa = rope(p[..., o_qa:o_qa + WIDTH_A].reshape(B, S, N_HEADS_A, HEAD_DIM_A), positions)
        ka = rope(p[..., o_ka:o_ka + WIDTH_A].reshape(B, S, N_HEADS_A, HEAD_DIM_A), positions)
        va = p[..., o_va:o_va + WIDTH_A].reshape(B, S, N_HEADS_A, HEAD_DIM_A)
        outs, lses = [], []
        for gi, (window, dilation) in enumerate(DILATED_GROUPS):
            hs = slice(gi * HEADS_PER_GROUP_A, (gi + 1) * HEADS_PER_GROUP_A)
            o_g_i, lse_g_i = dilated_window_attention(qa[:, :, hs], ka[:, :, hs], va[:, :, hs], window, dilation)
            outs.append(o_g_i)
            lses.append(lse_g_i)
        wgt = jax.nn.softmax(jnp.stack(lses, axis=0), axis=0)
        oa = jnp.sum(wgt[..., None] * jnp.stack(outs, axis=0), axis=0).astype(x.dtype)
        ya = oa.reshape(B, S, OUT_WIDTH_A) @ w_oa[l]

        cq = rms_norm(p[..., o_ql:o_ql + Q_LORA], norm_q[l]) @ w_uq[l]
        cq = cq.reshape(B, S, N_HEADS_B, QK_DIM_B)
        q_b = jnp.concatenate([cq[..., :QK_NOPE], rope(cq[..., QK_NOPE:], positions)], axis=-1)
        ckv = rms_norm(p[..., o_kvl:o_kvl + KV_LORA], norm_kv[l]) @ w_ukv[l]
        ckv = ckv.reshape(B, S, N_HEADS_B, QK_NOPE + V_DIM)
        k_pe = rope(p[..., o_kr:o_kr + QK_ROPE][:, :, None, :], positions)
        k_b = jnp.concatenate([ckv[..., :QK_NOPE],
                               jnp.broadcast_to(k_pe, (B, S, N_HEADS_B, QK_ROPE))], axis=-1)
        v_b = ckv[..., QK_NOPE:]
        ob = dense_attention(q_b, k_b, v_b, QK_DIM_B ** -0.5)
        yb = ob.reshape(B, S, OUT_WIDTH_B) @ w_ob[l]

        gates = jax.nn.sigmoid(p[..., o_g:o_g + N_BRANCH * D].reshape(B, S, N_BRANCH, D) + b_gate[l])
        merged = gates[:, :, 0] * ya + gates[:, :, 1] * yb
        x = x + merged @ w_out[l]

        h2 = rms_norm(x, norm_ffn[l])
        u = centred_depthwise_conv(h2 @ w_up[l], conv_w[l], conv_b[l])
        x = x + (jax.nn.silu(u[..., :D_FF]) * u[..., D_FF:]) @ w_down[l]

    return rms_norm(x, norm_final)
```

```python
import math
import numpy as np
import ml_dtypes
import concourse.bass as bass
import concourse.mybir as mybir
from concourse.bass_utils import run_bass_kernel_spmd

F32 = mybir.dt.float32
BF16 = mybir.dt.bfloat16
I32 = mybir.dt.int32
AF = mybir.ActivationFunctionType
ALU = mybir.AluOpType

NCORES = 8
S = 16384
TOK = S // NCORES
D = 2048
DC = D // 128
DEPTH = 2
WA = 1536
QL = 512
KVL = 512
KR = 64
NIN = 9792
O_QA, O_KA, O_VA = 0, 1536, 3072
O_QL = 4608
O_KVL = 5120
O_KR = 5632
O_G = 5696
NHB = 16
DFF = 5632
FC = DFF // 128
EPS = 1e-6
TWO_PI = 2.0 * math.pi


class Ctr:
    def __init__(self, nc, name):
        self.sem = nc.alloc_semaphore(name)
        self.n = 0
        self.name = name
        self.waiter = None


class Prog:
    CE = ("pe", "act", "dve")

    def __init__(self, nc):
        self.nc = nc
        self.eng = {"pe": nc.tensor, "act": nc.scalar, "dve": nc.vector,
                    "pool": nc.gpsimd, "sp": nc.sync}
        self.vc = {k: {} for k in self.eng}
        self.insts = {k: [] for k in self.CE}
        self.S = {}
        self.notes = {}
        self.lastw = {}
        self.readers = {}
        self.dma_ctrs = {}
        self.dma_vc = {}
        self.pe_relay = None
        self.nrelay = 0
        for p in self.CE:
            for c in ("pe", "act", "dve", "pool", "sp"):
                self.sem(p, c)
        self.free_ctrs = {"sp": [], "pool": []}
        self.nctr = 0
        for q, cnt in (("sp", 44), ("pool", 20)):
            for i in range(cnt):
                self.nctr += 1
                c = Ctr(nc, "d%s%d" % (q, self.nctr))
                c.uid = "d%d" % self.nctr
                c.q = q
                self.free_ctrs[q].append(c)
            self.free_ctrs[q].reverse()
        self.relay_buf = nc.alloc_sbuf_tensor("relay_buf", [128, 1008], F32)
        self.op("dve", lambda: nc.vector.memset(self.relay_buf[:], 0.0), writes=["relay_buf"])
        self.relay_w = nc.alloc_sbuf_tensor("relay_w", [128, 8], BF16)
        self.op("dve", lambda: nc.vector.memset(self.relay_w[:], 0.0), writes=["relay_w"])

    def sem(self, p, c):
        if (p, c) not in self.S:
            self.S[(p, c)] = Ctr(self.nc, "s_%s_%s" % (p, c))
            self.notes[(p, c)] = []
        return self.S[(p, c)]

    def dctr(self, name, q="sp"):
        if name not in self.dma_ctrs:
            assert self.free_ctrs[q], "out of DMA counters"
            c = self.free_ctrs[q].pop()
            self.dma_ctrs[name] = c
        c = self.dma_ctrs[name]
        assert c.q == q, (name, c.q, q)
        return c

    @staticmethod
    def _merge(a, b):
        for k, v in b.items():
            if a.get(k, 0) < v:
                a[k] = v

    def _find_direct(self, e, p, k):
        if (p, e) not in self.S:
            notes = []
        else:
            notes = self.notes[(p, e)]
        lo, hi = 0, len(notes)
        while lo < hi:
            mid = (lo + hi) // 2
            if notes[mid][0] >= k:
                hi = mid
            else:
                lo = mid + 1
        if lo < len(notes):
            return notes[lo]
        start = max(k, (notes[-1][0] + 1) if notes else 1)
        lst = self.insts[p]
        for j in range(start, min(len(lst), start + 64) + 1):
            if lst[j - 1][1] is None:
                return (j, None)
        return None

    def _first_knower(self, q, p, k):
        lst = self.insts[q]
        lo, hi = 0, len(lst)
        while lo < hi:
            mid = (lo + hi) // 2
            if lst[mid][2].get(p, 0) >= k:
                hi = mid
            else:
                lo = mid + 1
        return (lo + 1) if lo < len(lst) else None

    def _need_eng(self, e, p, k):
        if self.vc[e].get(p, 0) >= k:
            return
        src = p
        d = self._find_direct(e, p, k)
        if d is None:
            for q in self.CE:
                if q == p or q == e:
                    continue
                j = self._first_knower(q, p, k)
                if j is not None:
                    d2 = self._find_direct(e, q, j)
                    if d2 is not None:
                        src, d = q, d2
                        break
        if d is None:
            d = (self._relay(p), None)
        idx, val = d
        ctr = self.sem(src, e)
        if val is None:
            rec = self.insts[src][idx - 1]
            rec[0].then_inc(ctr.sem, 1)
            ctr.n += 1
            rec[1] = e
            val = ctr.n
            self.notes[(src, e)].append((idx, val))
        assert ctr.waiter in (None, e)
        ctr.waiter = e
        self.eng[e].wait_ge(ctr.sem, val)
        self._merge(self.vc[e], self.insts[src][idx - 1][2])

    def _relay(self, p):
        nc = self.nc
        self.nrelay += 1
        r = self.nrelay
        pb = 32 * (r % 4)
        col = 8 + (r // 4) % 1000
        if p == "dve":
            ins = nc.vector.memset(self.relay_buf[pb:pb + 1, col:col + 1], 0.0)
        elif p == "act":
            if not getattr(self, "relay_ready", False):
                self._need_eng("act", "dve", 1)
            ins = nc.scalar.copy(out=self.relay_buf[pb:pb + 1, col:col + 1], in_=self.relay_buf[0:1, 4:5])
        else:
            assert self.pe_relay is not None, "PE relay needed but no PSUM scratch configured"
            ps, lhsT, rhs = self.pe_relay
            ins = nc.tensor.matmul(ps, lhsT=lhsT, rhs=rhs, start=True, stop=True)
        vcd = dict(self.vc[p])
        self.insts[p].append([ins, None, vcd])
        vcd[p] = len(self.insts[p])
        return len(self.insts[p])

    def _need_dma(self, e, c, v):
        if self.vc[e].get(c.uid, 0) >= v:
            return
        assert c.waiter in (None, e), "DMA ctr %s waited by %s and %s" % (c.uid, c.waiter, e)
        c.waiter = e
        self.eng[e].wait_ge(c.sem, v)
        self._merge(self.vc[e], self.dma_vc[(c.uid, v)])
        self.vc[e][c.uid] = v

    def _need(self, e, dep):
        if dep is None:
            return
        if dep[0] == "eng":
            self._need_eng(e, dep[1], dep[2])
        else:
            self._need_dma(e, dep[1], dep[2])

    def _deps(self, e, reads, writes, pe_accum=False, same_dma=None):
        deps = []
        for k in reads:
            deps.append(self.lastw.get(k))
        for k in writes:
            lw = self.lastw.get(k)
            if pe_accum and lw is not None and lw[0] == "eng" and lw[1] == "pe":
                pass
            elif same_dma is not None and lw is not None and lw[0] == "dma" and lw[1] is self.dma_ctrs.get(same_dma):
                pass
            else:
                deps.append(lw)
            deps.extend(self.readers.get(k, {}).values())
        for d in deps:
            if d is not None and d[0] == "eng":
                self._need(e, d)
        for d in deps:
            if d is not None and d[0] != "eng":
                self._need(e, d)

    def _record(self, dep, reads, writes):
        src = (dep[0], dep[1] if dep[0] == "eng" else dep[1].uid)
        for k in reads:
            r = self.readers.setdefault(k, {})
            if src not in r or r[src][2] < dep[2]:
                r[src] = dep
        for k in writes:
            self.lastw[k] = dep
            self.readers[k] = {}

    def op(self, e, fn, reads=(), writes=()):
        self._deps(e, reads, writes, pe_accum=(e == "pe"))
        ins = fn()
        vcd = dict(self.vc[e])
        self.insts[e].append([ins, None, vcd])
        idx = len(self.insts[e])
        vcd[e] = idx
        dep = ("eng", e, idx)
        self._record(dep, reads, writes)
        return dep

    def dma(self, q, cname, out, in_, reads=(), writes=(), **kw):
        eq = {"aq": "act", "vq": "dve"}.get(q, q)
        self._deps(eq, reads, writes, same_dma=cname)
        c = self.dctr(cname, "sp" if q in ("aq", "vq") else q)
        ins = self.eng[eq].dma_start(out=out, in_=in_, **kw)
        ins.then_inc(c.sem, 16)
        c.n += 16
        self.dma_vc[(c.uid, c.n)] = dict(self.vc[eq])
        dep = ("dma", c, c.n)
        self._record(dep, reads, writes)
        return dep

    def finish(self, prefixes):
        for name, c in self.dma_ctrs.items():
            if any(name.startswith(p) for p in prefixes):
                e = c.waiter or "sp"
                if c.n > 0:
                    self._need_dma(e, c, c.n)

    def hard_reset(self, pe_scratch):
        nc = self.nc
        self.fence(pe_scratch)
        nc.all_engine_barrier()
        allc = list(self.S.values()) + self.free_ctrs["sp"] + list(self.dma_ctrs.values())
        for c in allc:
            if getattr(c, "q", None) == "pool":
                continue
            nc.gpsimd.sem_clear(c.sem)
            c.n = 0
            c.waiter = None
        nc.all_engine_barrier()
        self.notes = {k: [] for k in self.notes}
        self.insts = {k: [] for k in self.CE}
        self.vc = {k: {} for k in self.eng}
        self.dma_vc = {}
        self.relay_ready = True

    def fence(self, pe_scratch):
        nc = self.nc
        for name, c in self.dma_ctrs.items():
            if c.n > 0:
                self._need_dma(c.waiter or "dve", c, c.n)
        ps, pskey, w = pe_scratch
        self.op("pe", lambda: nc.tensor.matmul(ps, lhsT=w, rhs=w, start=True, stop=True), reads=["relay_w"], writes=[pskey])
        self._need_eng("dve", "pe", len(self.insts["pe"]))
        ia = self._relay("act")
        self._need_eng("dve", "act", ia)
        for e in ("pe", "act", "sp", "pool"):
            i = self._relay("dve")
            self._need_eng(e, "dve", i)
        self.lastw = {}
        self.readers = {}
        for c in self.dma_ctrs.values():
            c.waiter = None
            self.free_ctrs[c.q].append(c)
        self.dma_ctrs = {}


class Rot:
    def __init__(self, name, bufs):
        self.name = name
        self.bufs = bufs
        self.i = 0

    def next(self):
        j = self.i % len(self.bufs)
        self.i += 1
        return self.bufs[j], (self.name, j)


def _consts():
    p = np.arange(128)
    invf128 = (1.0 / (np.float32(10000.0) ** (np.arange(0, 128, 2, dtype=np.float32) / np.float32(128)))).astype(np.float32)
    invf64 = (1.0 / (np.float32(10000.0) ** (np.arange(0, 64, 2, dtype=np.float32) / np.float32(64)))).astype(np.float32)
    cst = np.zeros((128, 16), np.float32)
    sgn128 = np.where(p < 64, -1.0, 1.0).astype(np.float32)
    sgn64 = np.where((p % 64) < 32, -1.0, 1.0).astype(np.float32)
    cst[:, 0] = invf128[p % 64]
    cst[:, 1] = invf64[p % 32]
    cst[:, 2] = sgn128
    cst[:, 3] = sgn64
    cst[:, 4] = np.float32(math.pi / 2)
    cst[:, 5] = np.float32(EPS)
    perm128 = np.zeros((128, 128), np.float32)
    perm128[(p + 64) % 128, p] = 1.0
    perm64 = np.zeros((128, 128), np.float32)
    perm64[(p // 64) * 64 + ((p % 64) + 32) % 64, p] = 1.0
    return cst, perm128, perm64


class _Stop(Exception):
    pass


def build_phase_a(stop=None, ctx=None):
    if ctx is None:
        nc = bass.Bass("TRN2", target_bir_lowering=False)
        P = Prog(nc)
        sfx = ""
    else:
        nc, P, sfx = ctx.nc, ctx.P, ctx.sfx
    TT = 1024
    NTT = TOK // TT

    def din(name, shape, dt=F32):
        if ctx is not None:
            ap = ctx.t[name]
            assert list(ap.shape) == list(shape), (name, ap.shape, shape)
            return ap
        return nc.dram_tensor(name, shape, dt, kind="ExternalInput").ap()

    def dout(name, shape, dt=BF16):
        if ctx is not None:
            ap = ctx.t[name]
            assert list(ap.shape) == list(shape), (name, ap.shape, shape)
            return ap
        return nc.dram_tensor(name, shape, dt, kind="ExternalOutput").ap()

    xT = din("xT", [DC, 128, TOK])
    pos = din("pos", [128, TOK], I32)
    gmix = din("gmix", [128, DC])
    w_in = din("w_in", [D, NIN])
    nq = din("nq", [128, 4])
    nkv = din("nkv", [128, 4])
    w_uq = din("w_uq", [QL, 3072])
    w_ukv = din("w_ukv", [KVL, 4096])
    cst_d = din("cst", [128, 16])
    perm128_d = din("perm128", [128, 128])
    perm64_d = din("perm64", [128, 128])

    qaT = dout("qaT", [12, 128, TOK])
    kaT = dout("kaT", [12, 128, TOK])
    va = dout("va", [TOK, WA])
    QnT = dout("QnT", [NHB, 128, TOK])
    QrT = dout("QrT", [NHB // 2, 128, TOK])
    KnT = dout("KnT", [NHB, 128, TOK])
    kpeT = dout("kpeT", [64, TOK])
    Vb = dout("Vb", [NHB, 128, TOK // 128, 128])

    if ctx is None:
        sb = lambda name, shape, dt: nc.alloc_sbuf_tensor(name, shape, dt)
        psa = lambda name, shape, dt: nc.alloc_psum_tensor(name, shape, dt)
    else:
        sb, psa = ctx.sb, ctx.psa
    cst = sb("cst_sb", [128, 16], F32)
    cstA = sb("cstA_sb", [128, 16], F32)
    gm = sb("gm_sb", [128, DC], F32)
    nq_sb = sb("nq_sb", [128, 4], F32)
    nkv_sb = sb("nkv_sb", [128, 4], F32)
    perm128 = sb("perm128_sb", [128, 128], BF16)
    perm64 = sb("perm64_sb", [128, 128], BF16)
    ones = sb("ones_sb", [128, 128], F32)
    hT = sb("hT", [128, DC, TT], BF16)
    wbufs = Rot("wbuf", [sb(f"wbuf{i}", [128, DC, 512], BF16) for i in range(3)])
    xst = Rot("xst", [sb(f"xst{i}", [128, TT], F32) for i in range(2)])
    xsd = Rot("xsd", [sb(f"xsd{i}", [128, TT], F32) for i in range(2)])
    sqb = Rot("sqb", [sb(f"sqb{i}", [128, TT], F32) for i in range(2)])
    rstd = sb("rstd", [128, TT], F32)
    rtmp = sb("rtmp", [128, TT], F32)
    lat = sb("lat", [128, 4, TT], F32)
    cqn = sb("cqn", [128, 4, TT], BF16)
    ckvn = sb("ckvn", [128, 4, TT], BF16)
    posf = sb("posf", [128, TT], F32)
    posi = sb("posi", [128, TT], I32)
    ang = sb("ang", [128, TT], F32)
    qf = sb("qf", [128, TT], F32)
    cos128 = sb("cos128", [128, TT], F32)
    sin128 = sb("sin128", [128, TT], F32)
    cos64 = sb("cos64", [128, TT], F32)
    sin64 = sb("sin64", [128, TT], F32)
    xb = Rot("xb", [sb(f"xb{i}", [128, 512], BF16) for i in range(2)])
    t1 = Rot("t1", [sb(f"t1_{i}", [128, 512], F32) for i in range(2)])
    t2 = Rot("t2", [sb(f"t2_{i}", [128, 512], F32) for i in range(2)])
    ost = Rot("ost", [sb(f"ost{i}", [128, 512], BF16) for i in range(4)])
    ostA = Rot("ostA", [sb(f"ostA{i}", [128, 512], BF16) for i in range(4)])
    psum = Rot("ps", [psa(f"ps{i}", [128, 512], F32) for i in range(6)])
    psw = Rot("psw", [psa(f"psw{i}", [128, 512], F32) for i in range(2)])

    P.dma("sp", "k0", cst[:], cst_d, writes=["cst"])
    P.dma("sp", "k0a", cstA[:], cst_d, writes=["cstA"])
    P.dma("sp", "k1", gm[:], gmix, writes=["gm"])
    P.dma("sp", "k2", nq_sb[:], nq, writes=["nq"])
    P.dma("sp", "k3", nkv_sb[:], nkv, writes=["nkv"])
    if ctx is None:
        dummy = sb("dummy_sb", [128, 16], BF16)
        P.dma("pool", "kdummy", dummy[:], cst_d, writes=["dummy"])
    pst = sb("perm_stage", [128, 2, 128], F32)
    P.dma("sp", "k4", pst[:, 0, :], perm128_d, writes=["pst0"])
    P.dma("sp", "k5", pst[:, 1, :], perm64_d, writes=["pst1"])
    P.op("dve", lambda: nc.vector.tensor_copy(out=perm128[:], in_=pst[:, 0, :]), reads=["pst0"], writes=["perm128"])
    P.op("dve", lambda: nc.vector.tensor_copy(out=perm64[:], in_=pst[:, 1, :]), reads=["pst1"], writes=["perm64"])
    P.op("dve", lambda: nc.vector.memset(ones[:], 1.0), writes=["ones"])

    def load_w(dst_ap, src_ap, key):
        return P.dma("pool" if ctx is None else "sp", "w_" + str(key[1]), dst_ap, src_ap, writes=[key])

    def rope_tables(tt):
        P.dma("sp", "pos", posi[:], pos[:, tt * TT:(tt + 1) * TT], writes=["posi"])
        P.op("dve", lambda: nc.vector.tensor_copy(out=posf[:], in_=posi[:]), reads=["posi"], writes=["posf"])
        C1 = 6.28125
        C2 = TWO_PI - C1
        for (ci, cos_t, sin_t, cname, sname) in ((0, cos128, sin128, "cos128", "sin128"), (1, cos64, sin64, "cos64", "sin64")):
            P.op("dve", lambda: nc.vector.tensor_scalar(out=ang[:], in0=posf[:], scalar1=cst[:, ci:ci + 1], scalar2=None,
                                                        op0=ALU.mult), reads=["posf", "cst"], writes=["ang"])
            P.op("dve", lambda: nc.vector.tensor_scalar(out=qf[:], in0=ang[:], scalar1=1.0 / TWO_PI, scalar2=None,
                                                        op0=ALU.mult), reads=["ang"], writes=["qf"])
            P.op("dve", lambda: nc.vector.tensor_copy(out=posi[:], in_=qf[:]), reads=["qf"], writes=["posi"])
            P.op("dve", lambda: nc.vector.tensor_copy(out=qf[:], in_=posi[:]), reads=["posi"], writes=["qf"])
            P.op("dve", lambda: nc.vector.scalar_tensor_tensor(out=ang[:], in0=qf[:], scalar=-C1, in1=ang[:],
                                                               op0=ALU.mult, op1=ALU.add), reads=["qf", "ang"], writes=["ang"])
            P.op("dve", lambda: nc.vector.scalar_tensor_tensor(out=ang[:], in0=qf[:], scalar=-C2, in1=ang[:],
                                                               op0=ALU.mult, op1=ALU.add), reads=["qf", "ang"], writes=["ang"])
            P.op("dve", lambda: nc.vector.tensor_scalar(out=ang[:], in0=ang[:], scalar1=math.pi, scalar2=-math.pi,
                                                        op0=ALU.min, op1=ALU.max), reads=["ang"], writes=["ang"])
            P.op("act", lambda: nc.scalar.activation(out=sin_t[:], in_=ang[:], func=AF.Sin, scale=cstA[:, 2 + ci:3 + ci]),
                 reads=["ang", "cstA"], writes=[sname])
            P.op("act", lambda: nc.scalar.activation(out=qf[:], in_=ang[:], func=AF.Abs), reads=["ang"], writes=["qf"])
            P.op("act", lambda: nc.scalar.activation(out=cos_t[:], in_=qf[:], func=AF.Sin, scale=-1.0, bias=cstA[:, 4:5]),
                 reads=["qf", "cstA"], writes=[cname])

    def sumsq_rstd(src_list, nchunks):
        pw = []
        for hf in range(TT // 512):
            pw.append(psw.next())
        for c in range(nchunks):
            ap, key = src_list(c)
            sq, sqk = sqb.next()
            P.op("act", lambda: nc.scalar.activation(out=sq[:], in_=ap, func=AF.Square), reads=[key], writes=[sqk])
            for hf in range(TT // 512):
                pt, pk = pw[hf]
                P.op("pe", lambda: nc.tensor.matmul(pt[:], lhsT=ones[:], rhs=sq[:, hf * 512:(hf + 1) * 512],
                                                    start=(c == 0), stop=(c == nchunks - 1)),
                     reads=["ones", sqk], writes=[pk])
        for hf in range(TT // 512):
            pt, pk = pw[hf]
            P.op("act", lambda: nc.scalar.activation(out=rtmp[:, hf * 512:(hf + 1) * 512], in_=pt[:], func=AF.Sqrt,
                                                     scale=1.0 / (128 * nchunks), bias=cstA[:, 5:6]),
                 reads=[pk, "cstA"], writes=[("rtmp", hf)])
            P.op("dve", lambda: nc.vector.reciprocal(out=rstd[:, hf * 512:(hf + 1) * 512], in_=rtmp[:, hf * 512:(hf + 1) * 512]),
                 reads=[("rtmp", hf)], writes=[("rstd", hf)])

    def rope_evac(pt, pk, M, cos_t, sin_t, cname, sname, perm, permk, hf, dst_ap, dstk):
        xbt, xbk = xb.next()
        P.op("act", lambda: nc.scalar.copy(out=xbt[:M, :], in_=pt[:M, :]), reads=[pk], writes=[xbk])
        p2, p2k = psum.next()
        P.op("pe", lambda: nc.tensor.matmul(p2[:M, :], lhsT=perm[:M, :M], rhs=xbt[:M, :], start=True, stop=True),
             reads=[permk, xbk], writes=[p2k])
        a, ak = t1.next()
        b, bk = t2.next()
        cs = slice(hf * 512, (hf + 1) * 512)
        P.op("dve", lambda: nc.vector.tensor_tensor(out=a[:M, :], in0=pt[:M, :], in1=cos_t[:M, cs], op=ALU.mult),
             reads=[pk, cname], writes=[ak])
        P.op("dve", lambda: nc.vector.tensor_tensor(out=b[:M, :], in0=p2[:M, :], in1=sin_t[:M, cs], op=ALU.mult),
             reads=[p2k, sname], writes=[bk])
        P.op("dve", lambda: nc.vector.tensor_tensor(out=dst_ap, in0=a[:M, :], in1=b[:M, :], op=ALU.add),
             reads=[ak, bk], writes=[dstk])

    w_in_v = w_in.rearrange("(kc k) n -> k kc n", k=128)
    w_uq_v = w_uq.rearrange("(kc k) n -> k kc n", k=128)
    w_ukv_v = w_ukv.rearrange("(kc k) n -> k kc n", k=128)

    def ck(name):
        if stop == name:
            raise _Stop()

    try:
        for tt in range(NTT):
          tsl = slice(tt * TT, (tt + 1) * TT)
          rope_tables(tt)
          ck('rope')
          wq_t, wq_k = wbufs.next()
          load_w(wq_t[:], w_in_v[:, :, O_QL:O_QL + 512], wq_k)

          def xsrc(c):
              t, k = xst.next()
              P.dma("sp", "x_" + str(k[1]), t[:], xT[c, :, tsl], writes=[k])
              return t[:], k
          sumsq_rstd(xsrc, DC)
          for c in range(DC):
              t, k = xsd.next()
              P.dma("sp", "xd_" + str(k[1]), t[:], xT[c, :, tsl], writes=[k])
              P.op("dve", lambda: nc.vector.scalar_tensor_tensor(out=hT[:, c, :], in0=t[:], scalar=gm[:, c:c + 1], in1=rstd[:],
                                                                 op0=ALU.mult, op1=ALU.mult),
                   reads=[k, "gm", ("rstd", 0), ("rstd", 1)], writes=[("hT", c)])
          ck('norm')
          hkeys = [("hT", c) for c in range(DC)]

          def proj_fm(wt, wk, col0, M, hf, kch=DC, rhs=None, rkeys=None):
              pt, pk = psum.next()
              for kc in range(kch):
                  r = (hT if rhs is None else rhs)
                  P.op("pe", lambda: nc.tensor.matmul(pt[:M, :], lhsT=wt[:, kc, col0:col0 + M], rhs=r[:, kc, hf * 512:(hf + 1) * 512],
                                                      start=(kc == 0), stop=(kc == kch - 1)),
                       reads=[wk] + (hkeys if rkeys is None else rkeys), writes=[pk])
              return pt, pk

          for li, (o_l, nrm_sb, nrmk, dstn, dstk) in enumerate(((O_QL, nq_sb, "nq", cqn, "cqn"), (O_KVL, nkv_sb, "nkv", ckvn, "ckvn"))):
              if li == 0:
                  wt, wk = wq_t, wq_k
              else:
                  wt, wk = wbufs.next()
                  load_w(wt[:], w_in_v[:, :, o_l:o_l + 512], wk)
              for sub in range(4):
                  for hf in range(TT // 512):
                      pt, pk = proj_fm(wt, wk, sub * 128, 128, hf)
                      P.op("act", lambda: nc.scalar.copy(out=lat[:, sub, hf * 512:(hf + 1) * 512], in_=pt[:]),
                           reads=[pk], writes=[("lat", sub)])
              sumsq_rstd(lambda c: (lat[:, c, :], ("lat", c)), 4)
              for sub in range(4):
                  P.op("dve", lambda: nc.vector.scalar_tensor_tensor(out=dstn[:, sub, :], in0=lat[:, sub, :], scalar=nrm_sb[:, sub:sub + 1],
                                                                     in1=rstd[:], op0=ALU.mult, op1=ALU.mult),
                       reads=[("lat", sub), nrmk, ("rstd", 0), ("rstd", 1)], writes=[(dstk, sub)])
          ck('lat')
          wt, wk = wbufs.next()
          load_w(wt[:, :, 0:64], w_in_v[:, :, O_KR:O_KR + 64], wk)
          for hf in range(TT // 512):
              pt, pk = proj_fm(wt, wk, 0, 64, hf)
              o, ok = ost.next()
              rope_evac(pt, pk, 64, cos64, sin64, "cos64", "sin64", perm64, "perm64", hf, o[:64, :], ok)
              P.dma("aq" if ctx is not None else "sp", "o_" + str(ok[1]), kpeT[:, tt * TT + hf * 512: tt * TT + (hf + 1) * 512], o[:64, :], reads=[ok])

          ck('kpe')
          for (o_c, dst) in ((O_QA, qaT), (O_KA, kaT)):
              for blk in range(3):
                  wt, wk = wbufs.next()
                  load_w(wt[:], w_in_v[:, :, o_c + blk * 512:o_c + (blk + 1) * 512], wk)
                  for sub in range(4):
                      head = blk * 4 + sub
                      for hf in range(TT // 512):
                          pt, pk = proj_fm(wt, wk, sub * 128, 128, hf)
                          o, ok = ost.next()
                          import os
                          if os.environ.get("DBG") == "norope":
                              P.op("act", lambda: nc.scalar.copy(out=o[:], in_=pt[:]), reads=[pk], writes=[ok])
                          else:
                              if os.environ.get("DBG") == "perm64":
                                  rope_evac(pt, pk, 128, cos128, sin128, "cos128", "sin128", perm64, "perm64", hf, o[:], ok)
                              elif os.environ.get("DBG") == "cos64":
                                  rope_evac(pt, pk, 128, cos64, sin64, "cos64", "sin64", perm128, "perm128", hf, o[:], ok)
                              else:
                                  rope_evac(pt, pk, 128, cos128, sin128, "cos128", "sin128", perm128, "perm128", hf, o[:], ok)
                          if os.environ.get("DBG") != "nodma":
                              P.dma("aq" if ctx is not None else "sp", "o_" + str(ok[1]), dst[head, :, tt * TT + hf * 512: tt * TT + (hf + 1) * 512], o[:], reads=[ok])
                          ck('qa_%d_%d_%d' % (o_c, head, hf))
          ck('qaka')
          for blk in range(3):
              wt, wk = wbufs.next()
              load_w(wt[:], w_in_v[:, :, O_VA + blk * 512:O_VA + (blk + 1) * 512], wk)
              for tb in range(TT // 128):
                  pt, pk = psum.next()
                  for kc in range(DC):
                      P.op("pe", lambda: nc.tensor.matmul(pt[:], lhsT=hT[:, kc, tb * 128:(tb + 1) * 128], rhs=wt[:, kc, :],
                                                          start=(kc == 0), stop=(kc == DC - 1)),
                           reads=[wk] + hkeys, writes=[pk])
                  o, ok = ostA.next()
                  P.op("act", lambda: nc.scalar.copy(out=o[:], in_=pt[:]), reads=[pk], writes=[ok])
                  r0 = tt * TT + tb * 128
                  P.dma("aq" if ctx is not None else "sp", "oA_" + str(ok[1]), va[r0:r0 + 128, blk * 512:(blk + 1) * 512], o[:], reads=[ok])

          ck('va')
          cqk = [("cqn", c) for c in range(4)]
          for half in range(2):
              wt, wk = wbufs.next()
              wv = wt[:].rearrange("p a b -> p (a b)")[:, 0:4 * 1536].rearrange("p (a b) -> p a b", b=1536)
              src = w_uq_v[:, :, half * 1536:(half + 1) * 1536].rearrange("p a (h d) -> p a h d", d=192)
              for a_ in range(4):
                  load_w(wv[:, a_, 0:1024].rearrange("p (h d) -> p h d", d=128), src[:, a_, :, 0:128], wk)
                  load_w(wv[:, a_, 1024:1536].rearrange("p (h d) -> p h d", d=64), src[:, a_, :, 128:192], wk)
              for hl in range(8):
                  head = half * 8 + hl
                  for hf in range(TT // 512):
                      pt, pk = psum.next()
                      for kc in range(4):
                          P.op("pe", lambda: nc.tensor.matmul(pt[:], lhsT=wv[:, kc, hl * 128:(hl + 1) * 128], rhs=cqn[:, kc, hf * 512:(hf + 1) * 512],
                                                              start=(kc == 0), stop=(kc == 3)),
                               reads=[wk] + cqk, writes=[pk])
                      o, ok = ostA.next()
                      P.op("act", lambda: nc.scalar.copy(out=o[:], in_=pt[:]), reads=[pk], writes=[ok])
                      P.dma("aq" if ctx is not None else "sp", "oA_" + str(ok[1]), QnT[head, :, tt * TT + hf * 512: tt * TT + (hf + 1) * 512], o[:], reads=[ok])
              for pr in range(4):
                  pair = half * 4 + pr
                  for hf in range(TT // 512):
                      pt, pk = psum.next()
                      for kc in range(4):
                          P.op("pe", lambda: nc.tensor.matmul(pt[:], lhsT=wv[:, kc, 1024 + pr * 128:1024 + (pr + 1) * 128], rhs=cqn[:, kc, hf * 512:(hf + 1) * 512],
                                                              start=(kc == 0), stop=(kc == 3)),
                               reads=[wk] + cqk, writes=[pk])
                      o, ok = ost.next()
                      rope_evac(pt, pk, 128, cos64, sin64, "cos64", "sin64", perm64, "perm64", hf, o[:], ok)
                      P.dma("aq" if ctx is not None else "sp", "o_" + str(ok[1]), QrT[pair, :, tt * TT + hf * 512: tt * TT + (hf + 1) * 512], o[:], reads=[ok])

          ck('qup')
          ckk = [("ckvn", c) for c in range(4)]
          for half in range(2):
              wt, wk = wbufs.next()
              wv = wt[:].rearrange("p a b -> p (a b)").rearrange("p (a b) -> p a b", b=2048)
              src = w_ukv_v[:, :, half * 2048:(half + 1) * 2048].rearrange("p a (h d) -> p a h d", d=256)
              for a_ in range(4):
                  load_w(wv[:, a_, 0:1024].rearrange("p (h d) -> p h d", d=128), src[:, a_, :, 0:128], wk)
                  load_w(wv[:, a_, 1024:2048].rearrange("p (h d) -> p h d", d=128), src[:, a_, :, 128:256], wk)
              for hl in range(8):
                  head = half * 8 + hl
                  for hf in range(TT // 512):
                      pt, pk = psum.next()
                      for kc in range(4):
                          P.op("pe", lambda: nc.tensor.matmul(pt[:], lhsT=wv[:, kc, hl * 128:(hl + 1) * 128], rhs=ckvn[:, kc, hf * 512:(hf + 1) * 512],
                                                              start=(kc == 0), stop=(kc == 3)),
                               reads=[wk] + ckk, writes=[pk])
                      o, ok = ostA.next()
                      P.op("act", lambda: nc.scalar.copy(out=o[:], in_=pt[:]), reads=[pk], writes=[ok])
                      P.dma("aq" if ctx is not None else "sp", "oA_" + str(ok[1]), KnT[head, :, tt * TT + hf * 512: tt * TT + (hf + 1) * 512], o[:], reads=[ok])
              for hq in range(2):
                  for tb in range(TT // 128):
                      pt, pk = psum.next()
                      for kc in range(4):
                          P.op("pe", lambda: nc.tensor.matmul(pt[:],
                                                              lhsT=ckvn[:, kc, tb * 128:(tb + 1) * 128],
                                                              rhs=wv[:, kc, 1024 + hq * 512:1024 + (hq + 1) * 512],
                                                              start=(kc == 0), stop=(kc == 3)),
                               reads=[wk] + ckk, writes=[pk])
                      o, ok = ostA.next()
                      P.op("act", lambda: nc.scalar.copy(out=o[:], in_=pt[:]), reads=[pk], writes=[ok])
                      h0 = half * 8 + hq * 4
                      j = tt * (TT // 128) + tb
                      P.dma("aq" if ctx is not None else "sp", "oA_" + str(ok[1]), Vb[h0:h0 + 4, :, j, :].rearrange("h p d -> p h d"),
                            o[:].rearrange("p (h d) -> p h d", d=128), reads=[ok])

    except _Stop:
        pass
    if ctx is not None:
        return (psw.bufs[0][0:1, 0:1], ("psw", 0))
    P.finish(["o_", "oA_"])
    return nc


def _fm(v, nchunk):
    return np.ascontiguousarray(np.asarray(v, np.float32).reshape(nchunk, 128).T)


def run_phase_a(inputs, layer, xT_cores):
    cst, perm128, perm64 = _consts()
    nc = build_phase_a()
    pos = np.asarray(inputs["positions"]).reshape(S)
    in_maps = []
    for c in range(NCORES):
        in_maps.append({
            "xT": xT_cores[c],
            "pos": np.ascontiguousarray(np.broadcast_to(pos[c * TOK:(c + 1) * TOK][None, :], (128, TOK))).astype(np.int32),
            "gmix": _fm(inputs["norm_mix"][layer], DC),
            "w_in": np.asarray(inputs["w_in"][layer]),
            "nq": _fm(inputs["norm_q"][layer], 4),
            "nkv": _fm(inputs["norm_kv"][layer], 4),
            "w_uq": np.asarray(inputs["w_uq"][layer]),
            "w_ukv": np.asarray(inputs["w_ukv"][layer]),
            "cst": cst, "perm128": perm128, "perm64": perm64,
        })
    res = run_bass_kernel_spmd(nc, in_maps, core_ids=list(range(NCORES)))
    return res.results


DIL = (1, 4, 16)


def _dil_masks(core, ncores=NCORES):
    p = np.arange(128)[:, None]
    f = np.arange(128)[None, :]
    A = (p >= f).astype(np.float32)
    B = (p <= f).astype(np.float32)
    m = np.zeros((128, 4, 256), np.float32)
    for v in range(4):
        a = A.copy()
        b = B.copy()
        if (v & 1) and core == 0:
            a[:64, :] = 0.0
        if (v & 2) and core == ncores - 1:
            b[64:, :] = 0.0
        m[:, v, 0:128] = a
        m[:, v, 128:256] = b
    return m


def build_phase_b(stop=None, nheads=NHB, do_dil=True, ctx=None, ncores_b=NCORES):
    if ctx is None:
        nc = bass.Bass("TRN2", target_bir_lowering=False)
        P = Prog(nc)
        sfx = ""
    else:
        nc, P, sfx = ctx.nc, ctx.P, ctx.sfx

    def din(name, shape, dt=BF16):
        if ctx is not None:
            ap = ctx.t[name]
            assert list(ap.shape) == list(shape), (name, ap.shape, shape)
            return ap
        return nc.dram_tensor(name, shape, dt, kind="ExternalInput").ap()

    def dout(name, shape, dt=BF16):
        if ctx is not None:
            ap = ctx.t[name]
            assert list(ap.shape) == list(shape), (name, ap.shape, shape)
            return ap
        return nc.dram_tensor(name, shape, dt, kind="ExternalOutput").ap()

    QnT = din("QnT", [NHB, 128, TOK])
    QrT = din("QrT", [NHB, 64, TOK])
    KnT = din("KnT_all", [ncores_b, NHB, 128, TOK])
    kpeT = din("kpeT_all", [ncores_b, 64, TOK])
    Vb = din("Vb_all", [ncores_b, NHB, 128, TOK // 128, 128])
    qaT = din("qaT", [12, 128, TOK])
    kaH = din("kaT_halo", [12, 128, 2 * TOK])
    vaH = din("va_halo", [2 * TOK, WA])
    masks_d = din("masks", [128, 4, 256], F32)
    cst_d = din("cst", [128, 16], F32)
    obT = dout("obT", [NHB, 128, TOK])
    oaT = dout("oaT", [4, 128, TOK])

    if ctx is None:
        sb = lambda name, shape, dt: nc.alloc_sbuf_tensor(name, shape, dt)
        psa = lambda name, shape, dt: nc.alloc_psum_tensor(name, shape, dt)
    else:
        sb, psa = ctx.sb, ctx.psa
    if ctx is None:
        dummy = sb("dummy_sb", [128, 16], BF16)
        P.dma("pool", "kdummy", dummy[:], cst_d, writes=["dummy"])
    ones_f = sb("ones_f", [128, 128], F32)
    ones_b = sb("ones_b", [128, 128], BF16)
    P.op("dve", lambda: nc.vector.memset(ones_f[:], 1.0), writes=["ones_f"])
    P.op("dve", lambda: nc.vector.memset(ones_b[:], 1.0), writes=["ones_b"])
    mst = sb("mst", [128, 4, 256], F32)
    masks = sb("masks_sb", [128, 4, 256], BF16)
    P.dma("sp", "k0", mst[:], masks_d, writes=["mst"])
    P.op("dve", lambda: nc.vector.tensor_copy(out=masks[:], in_=mst[:]), reads=["mst"], writes=["masks"])

    kpe = sb("kpe_sb", [64, ncores_b, TOK], BF16)
    qn = Rot("qn", [sb(f"qn{i}", [128, TOK], BF16) for i in range(2)])
    qr = Rot("qr", [sb(f"qr{i}", [64, TOK], BF16) for i in range(2)])
    kn = Rot("kn", [sb(f"kn{i}", [128, TOK], BF16) for i in range(3)])
    vv = Rot("vv", [sb(f"vv{i}", [128, TOK // 128, 128], BF16) for i in range(3)])
    pT = Rot("pT", [sb(f"pT{i}", [128, 1024], BF16) for i in range(4)])
    dacc = [sb(f"dacc{i}", [128, 1024], F32) for i in range(2)]
    rden = Rot("rden", [sb(f"rden{i}", [128, 512], F32) for i in range(2)])
    ost = Rot("ost", [sb(f"ost{i}", [128, 512], BF16) for i in range(4)])
    acc = [psa(f"acc{i}", [128, 512], F32) for i in range(4)]
    sps = Rot("sps", [psa(f"sps{i}", [128, 1024], F32) for i in range(2)])

    SC_B = 192.0 ** -0.5
    SC_A = 128.0 ** -0.5

    for r in range(ncores_b):
        P.dma("sp", "kpe_%d" % r, kpe[:, r, :], kpeT[r], writes=[("kpe", r)])
    NQT = TOK // 512
    for h in range(nheads):
        qnt, qnk = qn.next()
        qrt, qrk = qr.next()
        P.dma("sp", "qn_" + str(qnk[1]), qnt[:], QnT[h], writes=[qnk])
        P.dma("sp", "qr_" + str(qrk[1]), qrt[:], QrT[h], writes=[qrk])
        steps = []
        for r in range(ncores_b):
            knt, knk = kn.next()
            vvt, vvk = vv.next()
            P.dma("sp", "kn_" + str(knk[1]), knt[:], KnT[r, h], writes=[knk])
            P.dma("pool" if ctx is None else "sp", "vv_" + str(vvk[1]), vvt[:], Vb[r, h], writes=[vvk])
            pend = None

            def qk(c, qp):
                st, sk = sps.next()
                for hf in range(2):
                    qt = qp * 2 + hf
                    P.op("pe", lambda: nc.tensor.matmul(st[:, hf * 512:(hf + 1) * 512], lhsT=knt[:, c * 128:(c + 1) * 128],
                                                        rhs=qnt[:, qt * 512:(qt + 1) * 512], start=True, stop=False),
                         reads=[knk, qnk], writes=[sk])
                    P.op("pe", lambda: nc.tensor.matmul(st[:, hf * 512:(hf + 1) * 512], lhsT=kpe[:, r, c * 128:(c + 1) * 128],
                                                        rhs=qrt[:, qt * 512:(qt + 1) * 512], start=False, stop=True),
                         reads=[("kpe", r), qrk], writes=[sk])
                return st, sk

            def rest(c, qp, st, sk):
                pt, pk = pT.next()
                P.op("act", lambda: nc.scalar.activation(out=pt[:], in_=st[:], func=AF.Exp, scale=SC_B), reads=[sk], writes=[pk])
                first = (r == 0 and c == 0)
                last = (r == ncores_b - 1 and c == TOK // 128 - 1)
                for hf in range(2):
                    qt = qp * 2 + hf
                    P.op("pe", lambda: nc.tensor.matmul(acc[qt][:], lhsT=vvt[:, c, :], rhs=pt[:, hf * 512:(hf + 1) * 512],
                                                        start=first, stop=last),
                         reads=[vvk, pk], writes=[("acc", qt)])
                if first:
                    P.op("dve", lambda: nc.vector.tensor_copy(out=dacc[qp][:], in_=pt[:]), reads=[pk], writes=[("dacc", qp)])
                else:
                    P.op("dve", lambda: nc.vector.tensor_tensor(out=dacc[qp][:], in0=dacc[qp][:], in1=pt[:], op=ALU.add),
                         reads=[pk, ("dacc", qp)], writes=[("dacc", qp)])

            order = [(c, qp) for c in range(TOK // 128) for qp in range(NQT // 2)]
            cur = qk(*order[0])
            for i, (c, qp) in enumerate(order):
                nxt = qk(*order[i + 1]) if i + 1 < len(order) else None
                rest(c, qp, *cur)
                cur = nxt
        for qt in range(NQT):
            dt_, dk = sps.next()
            P.op("pe", lambda: nc.tensor.matmul(dt_[:, 0:512], lhsT=ones_f[:], rhs=dacc[qt // 2][:, (qt % 2) * 512:(qt % 2 + 1) * 512],
                                                start=True, stop=True),
                 reads=["ones_f", ("dacc", qt // 2)], writes=[dk])
            rt, rk = rden.next()
            P.op("dve", lambda: nc.vector.reciprocal(out=rt[:], in_=dt_[:, 0:512]), reads=[dk], writes=[rk])
            o, ok = ost.next()
            P.op("dve", lambda: nc.vector.tensor_tensor(out=o[:], in0=acc[qt][:], in1=rt[:], op=ALU.mult),
                 reads=[("acc", qt), rk], writes=[ok])
            P.dma("aq" if ctx is not None else "sp", "o_" + str(ok[1]), obT[h, :, qt * 512:(qt + 1) * 512], o[:], reads=[ok])

    if do_dil:
        qa = sb("qa_sb", [128, TOK], BF16)
        ka = sb("ka_sb", [128, 2 * TOK], BF16)
        vt = Rot("vt", [sb(f"vt{i}", [128, 17, 128], BF16) for i in range(2)])
        nd = [sb(f"nd{i}", [128, 2, TOK], F32) for i in range(3)]
        rd2 = sb("rd2", [128, TOK], F32)
        pd = Rot("pd", [sb(f"pd{i}", [128, 256], BF16) for i in range(2)])
        pm = Rot("pm", [sb(f"pm{i}", [128, 256], BF16) for i in range(2)])
        oa_st = Rot("oast", [sb(f"oast{i}", [128, TOK], BF16) for i in range(2)])
        for j in range(4):
            for g, d in enumerate(DIL):
                head = g * 4 + j
                P.dma("sp", "qa", qa[:], qaT[head], writes=["qa"])
                P.dma("sp", "ka", ka[:], kaH[head], writes=["ka"])
                L2 = 2 * TOK // d
                NT = TOK // d // 128
                n_start = 1024 // d - 64
                for rr in range(d):
                    vtt, vtk = vt.next()
                    r0 = rr + d * n_start
                    src = vaH[r0: r0 + d * (128 * (NT + 1) - 1) + 1: d, head * 128:(head + 1) * 128]
                    src = src.rearrange("(k p) v -> p k v", p=128)
                    P.dma("sp", "vt_" + str(vtk[1]), vtt[:, 0:NT + 1, :], src, writes=[vtk])
                    for i in range(NT):
                        var = (1 if i == 0 else 0) | (2 if i == NT - 1 else 0)
                        st, sk = sps.next()
                        q0 = rr + d * 128 * i
                        qcols = qa[:, q0: q0 + d * 127 + 1: d]
                        for k2 in range(2):
                            k0 = rr + d * (n_start + 128 * (i + k2))
                            P.op("pe", lambda: nc.tensor.matmul(st[:, k2 * 128:(k2 + 1) * 128], lhsT=ka[:, k0: k0 + d * 127 + 1: d], rhs=qcols,
                                                                start=True, stop=True), reads=["ka", "qa"], writes=[sk])
                        pt, pk = pd.next()
                        P.op("act", lambda: nc.scalar.activation(out=pt[:], in_=st[:, 0:256], func=AF.Exp, scale=SC_A), reads=[sk], writes=[pk])
                        pmt, pmk = pm.next()
                        P.op("dve", lambda: nc.vector.tensor_tensor(out=pmt[:], in0=pt[:], in1=masks[:, var, :], op=ALU.mult),
                             reads=[pk, "masks"], writes=[pmk])
                        at, ak = sps.next()
                        for k2 in range(2):
                            P.op("pe", lambda: nc.tensor.matmul(at[:, 0:128], lhsT=vtt[:, i + k2, :], rhs=pmt[:, k2 * 128:(k2 + 1) * 128],
                                                                start=(k2 == 0), stop=(k2 == 1)), reads=[vtk, pmk], writes=[ak])
                        for k2 in range(2):
                            P.op("pe", lambda: nc.tensor.matmul(at[:, 128:256], lhsT=ones_b[:], rhs=pmt[:, k2 * 128:(k2 + 1) * 128],
                                                                start=(k2 == 0), stop=(k2 == 1)), reads=["ones_b", pmk], writes=[ak])
                        P.op("act", lambda: nc.scalar.copy(out=nd[g][:, :, q0: q0 + d * 127 + 1: d],
                                                           in_=at[:, 0:256].rearrange("p (a b) -> p a b", b=128)),
                             reads=[ak], writes=[("nd", g)])
            P.op("dve", lambda: nc.vector.tensor_tensor(out=nd[0][:], in0=nd[0][:], in1=nd[1][:], op=ALU.add),
                 reads=[("nd", 0), ("nd", 1)], writes=[("nd", 0)])
            P.op("dve", lambda: nc.vector.tensor_tensor(out=nd[0][:], in0=nd[0][:], in1=nd[2][:], op=ALU.add),
                 reads=[("nd", 0), ("nd", 2)], writes=[("nd", 0)])
            P.op("dve", lambda: nc.vector.reciprocal(out=rd2[:], in_=nd[0][:, 1, :]), reads=[("nd", 0)], writes=["rd2"])
            o, ok = oa_st.next()
            P.op("dve", lambda: nc.vector.tensor_tensor(out=o[:], in0=nd[0][:, 0, :], in1=rd2[:], op=ALU.mult),
                 reads=[("nd", 0), "rd2"], writes=[ok])
            P.dma("aq" if ctx is not None else "sp", "oa_" + str(ok[1]), oaT[j], o[:], reads=[ok])

    if ctx is not None:
        return (sps.bufs[0][0:1, 0:1], ("sps", 0))
    P.finish(["o_", "oa_"])
    return nc


class NormHelper:
    def __init__(self, nc, P, TT, ones, cst, psw, sqb, rtmp, rstd, tag=""):
        self.nc, self.P, self.TT = nc, P, TT
        self.ones, self.cst, self.psw, self.sqb, self.rtmp, self.rstd = ones, cst, psw, sqb, rtmp, rstd
        self.tag = tag
        self.pw = None

    def begin(self):
        self.pw = [self.psw.next() for _ in range(self.TT // 512)]

    def add(self, ap, key, c, nchunks):
        nc, P = self.nc, self.P
        sq, sqk = self.sqb.next()
        P.op("act", lambda: nc.scalar.activation(out=sq[:], in_=ap, func=AF.Square), reads=[key], writes=[sqk])
        for hf in range(self.TT // 512):
            pt, pk = self.pw[hf]
            P.op("pe", lambda: nc.tensor.matmul(pt[:], lhsT=self.ones[:], rhs=sq[:, hf * 512:(hf + 1) * 512],
                                                start=(c == 0), stop=(c == nchunks - 1)),
                 reads=["ones", sqk], writes=[pk])

    def finish(self, nchunks):
        nc, P = self.nc, self.P
        for hf in range(self.TT // 512):
            pt, pk = self.pw[hf]
            sl = slice(hf * 512, (hf + 1) * 512)
            P.op("act", lambda: nc.scalar.activation(out=self.rtmp[:, sl], in_=pt[:], func=AF.Sqrt,
                                                     scale=1.0 / (128 * nchunks), bias=self.cst[:, 5:6]),
                 reads=[pk, "cst"], writes=[("rtmp" + self.tag, hf)])
            P.op("dve", lambda: nc.vector.reciprocal(out=self.rstd[:, sl], in_=self.rtmp[:, sl]),
                 reads=[("rtmp" + self.tag, hf)], writes=[("rstd" + self.tag, hf)])
        return [("rstd" + self.tag, hf) for hf in range(self.TT // 512)]


def build_phase_c1(ctx=None):
    if ctx is None:
        nc = bass.Bass("TRN2", target_bir_lowering=False)
        P = Prog(nc)
        sfx = ""
    else:
        nc, P, sfx = ctx.nc, ctx.P, ctx.sfx
    TT = 512
    NTT = TOK // TT

    def din(name, shape, dt=F32):
        if ctx is not None:
            ap = ctx.t[name]
            assert list(ap.shape) == list(shape), (name, ap.shape, shape)
            return ap
        return nc.dram_tensor(name, shape, dt, kind="ExternalInput").ap()

    def dout(name, shape, dt=BF16):
        if ctx is not None:
            ap = ctx.t[name]
            assert list(ap.shape) == list(shape), (name, ap.shape, shape)
            return ap
        return nc.dram_tensor(name, shape, dt, kind="ExternalOutput").ap()

    xT = din("xT", [DC, 128, TOK])
    gmix = din("gmix", [128, DC])
    gffn = din("gffn", [128, DC])
    bg = din("bgate", [128, 2, DC])
    w_in = din("w_in", [D, NIN])
    w_oa = din("w_oa", [512, D])
    w_ob = din("w_ob", [D, D])
    w_out = din("w_out", [D, D])
    oaT = din("oaT", [4, 128, TOK], BF16)
    obT = din("obT", [NHB, 128, TOK], BF16)
    cst_d = din("cst", [128, 16])
    xmT = dout("xmT", [DC, 128, TOK], F32)
    h2T = dout("h2T", [DC, 128, TOK], BF16)

    if ctx is None:
        sb = lambda name, shape, dt: nc.alloc_sbuf_tensor(name, shape, dt)
        psa = lambda name, shape, dt: nc.alloc_psum_tensor(name, shape, dt)
    else:
        sb, psa = ctx.sb, ctx.psa
    if ctx is None:
        dummy = sb("dummy_sb", [128, 16], BF16)
        P.dma("pool", "kdummy", dummy[:], cst_d, writes=["dummy"])
    cst = sb("cst_sb", [128, 16], F32)
    gm = sb("gm_sb", [128, DC], F32)
    gf = sb("gf_sb", [128, DC], F32)
    bgs = sb("bg_sb", [128, 2, DC], F32)
    ones = sb("ones_sb", [128, 128], F32)
    P.dma("sp", "k0", cst[:], cst_d, writes=["cst"])
    P.dma("sp", "k1", gm[:], gmix, writes=["gm"])
    P.dma("sp", "k2", gf[:], gffn, writes=["gf"])
    P.dma("sp", "k3", bgs[:], bg, writes=["bg"])
    P.op("dve", lambda: nc.vector.memset(ones[:], 1.0), writes=["ones"])

    hT = sb("hT", [128, DC, TT], BF16)
    obs = sb("obs", [128, NHB, TT], BF16)
    oas = sb("oas", [128, 4, TT], BF16)
    mg = sb("mg", [128, DC, TT], BF16)
    xm = sb("xm", [128, DC, TT], F32)
    sA = sb("sA", [128, 4, TT], F32)
    mA = sb("mA", [128, 4, TT], F32)
    tB = Rot("tB", [sb(f"tB{i}", [128, TT], F32) for i in range(2)])
    wbufs = Rot("wbuf", [sb(f"wbuf{i}", [128, DC, 512], BF16) for i in range(3)])
    xst = Rot("xst", [sb(f"xst{i}", [128, TT], F32) for i in range(3)])
    xsd = Rot("xsd", [sb(f"xsd{i}", [128, TT], F32) for i in range(3)])
    sqb = Rot("sqb", [sb(f"sqb{i}", [128, TT], F32) for i in range(2)])
    rstd = sb("rstd", [128, TT], F32)
    rtmp = sb("rtmp", [128, TT], F32)
    ost = Rot("ost", [sb(f"ost{i}", [128, TT], BF16) for i in range(3)])
    psum = Rot("ps", [psa(f"ps{i}", [128, 512], F32) for i in range(6)])
    psw = Rot("psw", [psa(f"psw{i}", [128, 512], F32) for i in range(2)])
    NH = NormHelper(nc, P, TT, ones, cst, psw, sqb, rtmp, rstd)

    w_in_v = w_in.rearrange("(kc k) n -> k kc n", k=128)
    w_oa_v = w_oa.rearrange("(kc k) n -> k kc n", k=128)
    w_ob_v = w_ob.rearrange("(kc k) n -> k kc n", k=128)
    w_out_v = w_out.rearrange("(kc k) n -> k kc n", k=128)

    def load_w(dst_ap, src_ap, key):
        return P.dma("pool" if ctx is None else "sp", "w_" + str(key[1]), dst_ap, src_ap, writes=[key])

    def mm_group(wt, wk, col0, rhs, rkeys, kch):
        pt, pk = psum.next()
        for kc in range(kch):
            P.op("pe", lambda: nc.tensor.matmul(pt[:], lhsT=wt[:, kc, col0:col0 + 128], rhs=rhs[:, kc, :],
                                                start=(kc == 0), stop=(kc == kch - 1)), reads=[wk] + rkeys, writes=[pk])
        return pt, pk

    for tt in range(NTT):
        tsl = slice(tt * TT, (tt + 1) * TT)
        P.dma("sp", "ob", obs[:], obT[:, :, tsl].rearrange("h p t -> p h t"), writes=["obs"])
        P.dma("sp", "oa", oas[:], oaT[:, :, tsl].rearrange("h p t -> p h t"), writes=["oas"])
        NH.begin()
        for c in range(DC):
            t, k = xst.next()
            P.dma("sp", "x_" + str(k[1]), t[:], xT[c, :, tsl], writes=[k])
            NH.add(t[:], k, c, DC)
        rk = NH.finish(DC)
        for c in range(DC):
            t, k = xsd.next()
            P.dma("sp", "xd_" + str(k[1]), t[:], xT[c, :, tsl], writes=[k])
            P.op("dve", lambda: nc.vector.scalar_tensor_tensor(out=hT[:, c, :], in0=t[:], scalar=gm[:, c:c + 1], in1=rstd[:],
                                                               op0=ALU.mult, op1=ALU.mult),
                 reads=[k, "gm"] + rk, writes=[("hT", c)])
        hkeys = [("hT", c) for c in range(DC)]
        for fbq in range(4):
            wt, wk = wbufs.next()
            load_w(wt[:], w_in_v[:, :, O_G + fbq * 512: O_G + (fbq + 1) * 512], wk)
            for sub in range(4):
                fb = fbq * 4 + sub
                pt, pk = mm_group(wt, wk, sub * 128, hT, hkeys, DC)
                P.op("act", lambda: nc.scalar.activation(out=sA[:, sub, :], in_=pt[:], func=AF.Sigmoid, bias=bgs[:, 0, fb:fb + 1]),
                     reads=[pk, "bg"], writes=[("sA", sub)])
            wt, wk = wbufs.next()
            load_w(wt[:, 0:4, :], w_oa_v[:, :, fbq * 512:(fbq + 1) * 512], wk)
            for sub in range(4):
                pt, pk = mm_group(wt, wk, sub * 128, oas, ["oas"], 4)
                P.op("dve", lambda: nc.vector.tensor_tensor(out=mA[:, sub, :], in0=pt[:], in1=sA[:, sub, :], op=ALU.mult),
                     reads=[pk, ("sA", sub)], writes=[("mA", sub)])
            wt, wk = wbufs.next()
            load_w(wt[:], w_in_v[:, :, O_G + D + fbq * 512: O_G + D + (fbq + 1) * 512], wk)
            for sub in range(4):
                fb = fbq * 4 + sub
                pt, pk = mm_group(wt, wk, sub * 128, hT, hkeys, DC)
                P.op("act", lambda: nc.scalar.activation(out=sA[:, sub, :], in_=pt[:], func=AF.Sigmoid, bias=bgs[:, 1, fb:fb + 1]),
                     reads=[pk, "bg"], writes=[("sA", sub)])
            wt, wk = wbufs.next()
            load_w(wt[:], w_ob_v[:, :, fbq * 512:(fbq + 1) * 512], wk)
            for sub in range(4):
                fb = fbq * 4 + sub
                pt, pk = mm_group(wt, wk, sub * 128, obs, ["obs"], DC)
                tb, tbk = tB.next()
                P.op("dve", lambda: nc.vector.tensor_tensor(out=tb[:], in0=pt[:], in1=sA[:, sub, :], op=ALU.mult),
                     reads=[pk, ("sA", sub)], writes=[tbk])
                P.op("dve", lambda: nc.vector.tensor_tensor(out=mg[:, fb, :], in0=tb[:], in1=mA[:, sub, :], op=ALU.add),
                     reads=[tbk, ("mA", sub)], writes=[("mg", fb)])
        mkeys = [("mg", c) for c in range(DC)]
        NH.begin()
        for fbq in range(4):
            wt, wk = wbufs.next()
            load_w(wt[:], w_out_v[:, :, fbq * 512:(fbq + 1) * 512], wk)
            for sub in range(4):
                fb = fbq * 4 + sub
                pt, pk = mm_group(wt, wk, sub * 128, mg, mkeys, DC)
                t, k = xsd.next()
                P.dma("sp", "xd_" + str(k[1]), t[:], xT[fb, :, tsl], writes=[k])
                P.op("dve", lambda: nc.vector.tensor_tensor(out=xm[:, fb, :], in0=pt[:], in1=t[:], op=ALU.add),
                     reads=[pk, k], writes=[("xm", fb)])
                P.dma("aq" if ctx is not None else "sp", "xm_%d" % fb, xmT[fb, :, tsl], xm[:, fb, :], reads=[("xm", fb)])
                NH.add(xm[:, fb, :], ("xm", fb), fb, DC)
        rk = NH.finish(DC)
        for c in range(DC):
            o, ok = ost.next()
            P.op("dve", lambda: nc.vector.scalar_tensor_tensor(out=o[:], in0=xm[:, c, :], scalar=gf[:, c:c + 1], in1=rstd[:],
                                                               op0=ALU.mult, op1=ALU.mult),
                 reads=[("xm", c), "gf"] + rk, writes=[ok])
            P.dma("aq" if ctx is not None else "sp", "o_" + str(ok[1]), h2T[c, :, tsl], o[:], reads=[ok])

    if ctx is not None:
        return (psw.bufs[0][0:1, 0:1], ("psw", 0))
    P.finish(["o_", "xm_"])
    return nc


def build_phase_c2(final, ctx=None):
    if ctx is None:
        nc = bass.Bass("TRN2", target_bir_lowering=False)
        P = Prog(nc)
        sfx = ""
    else:
        nc, P, sfx = ctx.nc, ctx.P, ctx.sfx
    TT = 512
    NTT = TOK // TT

    def din(name, shape, dt=F32):
        if ctx is not None:
            ap = ctx.t[name]
            assert list(ap.shape) == list(shape), (name, ap.shape, shape)
            return ap
        return nc.dram_tensor(name, shape, dt, kind="ExternalInput").ap()

    def dout(name, shape, dt=BF16):
        if ctx is not None:
            ap = ctx.t[name]
            assert list(ap.shape) == list(shape), (name, ap.shape, shape)
            return ap
        return nc.dram_tensor(name, shape, dt, kind="ExternalOutput").ap()

    h2x_d = din("h2x", [DC, 128, TOK + 2], BF16)
    xmT = din("xmT", [DC, 128, TOK])
    w_up = din("w_up", [D, 2 * DFF])
    w_down = din("w_down", [DFF, D])
    cw = din("conv_w", [128, 3, 2 * FC])
    cb = din("conv_b", [128, 2 * FC])
    gfin = din("gfin", [128, DC])
    cst_d = din("cst", [128, 16])
    xoT = dout("xoT", [DC, 128, TOK], F32)

    if ctx is None:
        sb = lambda name, shape, dt: nc.alloc_sbuf_tensor(name, shape, dt)
        psa = lambda name, shape, dt: nc.alloc_psum_tensor(name, shape, dt)
    else:
        sb, psa = ctx.sb, ctx.psa
    if ctx is None:
        dummy = sb("dummy_sb", [128, 16], BF16)
        P.dma("pool", "kdummy", dummy[:], cst_d, writes=["dummy"])
    cst = sb("cst_sb", [128, 16], F32)
    cws = sb("cw_sb", [128, 3, 2 * FC], F32)
    cbs = sb("cb_sb", [128, 2 * FC], F32)
    gfs = sb("gfin_sb", [128, DC], F32)
    ones = sb("ones_sb", [128, 128], F32)
    P.dma("sp", "k0", cst[:], cst_d, writes=["cst"])
    P.dma("sp", "k1", cws[:], cw, writes=["cw"])
    P.dma("sp", "k2", cbs[:], cb, writes=["cb"])
    P.dma("sp", "k3", gfs[:], gfin, writes=["gfin"])
    P.op("dve", lambda: nc.vector.memset(ones[:], 1.0), writes=["ones"])

    h2x = sb("h2x_sb", [128, DC, TT + 2], BF16)
    gT = sb("gT", [128, FC, TT], BF16)
    wbufs = Rot("wbuf", [sb(f"wbuf{i}", [128, DC, 512], BF16) for i in range(4)])
    uext = Rot("uext", [sb(f"uext{i}", [128, TT + 2], F32) for i in range(4)])
    ua = Rot("ua", [sb(f"ua{i}", [128, TT], F32) for i in range(2)])
    ub = Rot("ub", [sb(f"ub{i}", [128, TT], F32) for i in range(2)])
    sa = Rot("sa", [sb(f"sa{i}", [128, TT], F32) for i in range(2)])
    xst = Rot("xst", [sb(f"xst{i}", [128, TT], F32) for i in range(2)])
    xo = sb("xo", [128, DC, TT], F32)
    sqb = Rot("sqb", [sb(f"sqb{i}", [128, TT], F32) for i in range(2)])
    rstd = sb("rstd", [128, TT], F32)
    rtmp = sb("rtmp", [128, TT], F32)
    ost = Rot("ost", [sb(f"ost{i}", [128, TT], F32) for i in range(2)])
    pmain = Rot("pm", [psa(f"pm{i}", [128, 512], F32) for i in range(2)])
    phalo = Rot("ph", [psa(f"ph{i}", [128, 512], F32) for i in range(2)])
    pdown = [psa(f"pd{i}", [128, 512], F32) for i in range(4)]
    NH = NormHelper(nc, P, TT, ones, cst, pmain, sqb, rtmp, rstd)

    w_up_v = w_up.rearrange("(kc k) n -> k kc n", k=128)
    w_dn_v = w_down.rearrange("(kc k) n -> k kc n", k=128)

    def load_w(dst_ap, src_ap, key):
        return P.dma("pool" if ctx is None else "sp", "w_" + str(key[1]), dst_ap, src_ap, writes=[key])

    hkeys = ["h2x"]
    halo_slot = [0]

    def up_conv(wt, wk, sub, chunk, dst_rot):
        pt, pk = pmain.next()
        for kc in range(DC):
            P.op("pe", lambda: nc.tensor.matmul(pt[:], lhsT=wt[:, kc, sub * 128:(sub + 1) * 128], rhs=h2x[:, kc, 1:TT + 1],
                                                start=(kc == 0), stop=(kc == DC - 1)), reads=[wk] + hkeys, writes=[pk])
        ph, phk = phalo.next()
        hs = 0
        phv = ph[:, 0:2]
        for kc in range(DC):
            P.op("pe", lambda: nc.tensor.matmul(phv, lhsT=wt[:, kc, sub * 128:(sub + 1) * 128], rhs=h2x[:, kc, 0:TT + 2:TT + 1],
                                                start=(kc == 0), stop=(kc == DC - 1)), reads=[wk] + hkeys, writes=[phk])
        ue, uek = uext.next()
        P.op("act", lambda: nc.scalar.copy(out=ue[:, 1:TT + 1], in_=pt[:]), reads=[pk], writes=[uek])
        P.op("act", lambda: nc.scalar.copy(out=ue[:, 0:TT + 2:TT + 1], in_=phv), reads=[phk, uek], writes=[uek])
        u, uk = dst_rot.next()
        P.op("dve", lambda: nc.vector.tensor_scalar(out=u[:], in0=ue[:, 1:TT + 1], scalar1=cws[:, 1, chunk:chunk + 1],
                                                    scalar2=cbs[:, chunk:chunk + 1], op0=ALU.mult, op1=ALU.add),
             reads=[uek, "cw", "cb"], writes=[uk])
        P.op("dve", lambda: nc.vector.scalar_tensor_tensor(out=u[:], in0=ue[:, 0:TT], scalar=cws[:, 0, chunk:chunk + 1], in1=u[:],
                                                           op0=ALU.mult, op1=ALU.add), reads=[uek, "cw", uk], writes=[uk])
        P.op("dve", lambda: nc.vector.scalar_tensor_tensor(out=u[:], in0=ue[:, 2:TT + 2], scalar=cws[:, 2, chunk:chunk + 1], in1=u[:],
                                                           op0=ALU.mult, op1=ALU.add), reads=[uek, "cw", uk], writes=[uk])
        return u, uk

    for tt in range(NTT):
        tsl = slice(tt * TT, (tt + 1) * TT)
        P.dma("sp", "h2x", h2x[:], h2x_d[:, :, tt * TT: tt * TT + TT + 2].rearrange("c p t -> p c t"), writes=["h2x"])
        for ib in range(DFF // 512):
            wa, wak = wbufs.next()
            load_w(wa[:], w_up_v[:, :, ib * 512:(ib + 1) * 512], wak)
            wb_, wbk = wbufs.next()
            load_w(wb_[:], w_up_v[:, :, DFF + ib * 512: DFF + (ib + 1) * 512], wbk)
            for sub in range(4):
                j = ib * 4 + sub
                u_a, uak = up_conv(wa, wak, sub, j, ua)
                u_b, ubk = up_conv(wb_, wbk, sub, FC + j, ub)
                s_, sk = sa.next()
                P.op("act", lambda: nc.scalar.activation(out=s_[:], in_=u_a[:], func=AF.Silu), reads=[uak], writes=[sk])
                P.op("dve", lambda: nc.vector.tensor_tensor(out=gT[:, j, :], in0=s_[:], in1=u_b[:], op=ALU.mult),
                     reads=[sk, ubk], writes=[("gT", j)])
        gkeys = [("gT", j) for j in range(FC)]
        if final:
            NH.begin()
        for fbq in range(4):
            for kg in range(4):
                wt, wk = wbufs.next()
                load_w(wt[:, 0:11, :], w_dn_v[:, kg * 11:(kg + 1) * 11, fbq * 512:(fbq + 1) * 512], wk)
                for sub in range(4):
                    for kc in range(11):
                        P.op("pe", lambda: nc.tensor.matmul(pdown[sub][:], lhsT=wt[:, kc, sub * 128:(sub + 1) * 128], rhs=gT[:, kg * 11 + kc, :],
                                                            start=(kg == 0 and kc == 0), stop=(kg == 3 and kc == 10)),
                             reads=[wk] + gkeys, writes=[("pdown", sub)])
            for sub in range(4):
                fb = fbq * 4 + sub
                t, k = xst.next()
                P.dma("sp", "x_" + str(k[1]), t[:], xmT[fb, :, tsl], writes=[k])
                if final:
                    P.op("dve", lambda: nc.vector.tensor_tensor(out=xo[:, fb, :], in0=pdown[sub][:], in1=t[:], op=ALU.add),
                         reads=[("pdown", sub), k], writes=[("xo", fb)])
                    NH.add(xo[:, fb, :], ("xo", fb), fb, DC)
                else:
                    o, ok = ost.next()
                    P.op("dve", lambda: nc.vector.tensor_tensor(out=o[:], in0=pdown[sub][:], in1=t[:], op=ALU.add),
                         reads=[("pdown", sub), k], writes=[ok])
                    P.dma("aq" if ctx is not None else "sp", "o_" + str(ok[1]), xoT[fb, :, tsl], o[:], reads=[ok])
        if final:
            rk = NH.finish(DC)
            for c in range(DC):
                o, ok = ost.next()
                P.op("dve", lambda: nc.vector.scalar_tensor_tensor(out=o[:], in0=xo[:, c, :], scalar=gfs[:, c:c + 1], in1=rstd[:],
                                                                   op0=ALU.mult, op1=ALU.mult),
                     reads=[("xo", c), "gfin"] + rk, writes=[ok])
                P.dma("aq" if ctx is not None else "sp", "o_" + str(ok[1]), xoT[c, :, tsl], o[:], reads=[ok])

    if ctx is not None:
        return (pmain.bufs[0][0:1, 0:1], ("pm", 0))
    P.finish(["o_"])
    return nc


class Ctx:
    ARENA = 198 * 1024

    def __init__(self, nc, P):
        self.nc, self.P = nc, P
        self.t = {}
        self.sfx = ""
        self.arena = nc.alloc_sbuf_tensor("arena", [128, self.ARENA // 2], BF16)
        self.banks = [nc.alloc_psum_tensor("bank%d" % i, [128, 1024], F32) for i in range(4)]
        self.off = 0
        self.nb = 0

    def begin(self):
        self.off = 0
        self.nb = 0

    def sb(self, name, shape, dt):
        esz = 4 if dt in (F32, I32) else 2
        n = 1
        for d in shape[1:]:
            n *= d
        nbytes = (n * esz + 31) // 32 * 32
        assert self.off + nbytes <= self.ARENA, ("arena overflow", name, self.off, nbytes)
        v = self.arena[0:shape[0], self.off // 2:(self.off + n * esz) // 2]
        if dt != BF16:
            v = v.bitcast(dt)
        self.off += nbytes
        if len(shape) == 3:
            v = v.rearrange("p (a b) -> p a b", b=shape[2])
        elif len(shape) == 4:
            v = v.rearrange("p (a b c) -> p a b c", b=shape[2], c=shape[3])
        return v

    def psa(self, name, shape, dt):
        assert dt == F32 and list(shape) in ([128, 512], [128, 1024])
        if shape[1] == 1024:
            self.nb = (self.nb + 1) // 2 * 2
            b = self.banks[self.nb // 2][:]
            self.nb += 2
            return b
        b = self.banks[self.nb // 2][:, (self.nb % 2) * 512:(self.nb % 2 + 1) * 512]
        self.nb += 1
        return b


def build_fused(nvc=NCORES, depth=DEPTH):
    nc = bass.Bass("TRN2", target_bir_lowering=False)
    P = Prog(nc)
    ctx = Ctx(nc, P)
    E4 = [DC, 128, TOK]

    def ext(name, shape, dt=F32):
        return nc.dram_tensor(name, shape, dt, kind="ExternalInput").ap()

    def scr(name, shape, dt=BF16):
        return nc.dram_tensor(name, shape, dt).ap()

    xT0 = ext("xT", [nvc] + E4)
    pos = ext("pos", [nvc, 64, TOK], I32)
    masks = ext("masks", [nvc, 128, 4, 256])
    cst_d = ext("cst", [128, 16])
    perm128_d = ext("perm128", [128, 128])
    perm64_d = ext("perm64", [128, 128])
    gmix = ext("gmix", [depth, 128, DC])
    gffn = ext("gffn", [depth, 128, DC])
    gfin = ext("gfin", [128, DC])
    bgate = ext("bgate", [depth, 128, 2, DC])
    nq = ext("nq", [depth, 128, 4])
    nkv = ext("nkv", [depth, 128, 4])
    cw = ext("conv_w", [depth, 128, 3, 2 * FC])
    cb = ext("conv_b", [depth, 128, 2 * FC])
    w_in = ext("w_in", [depth, D, NIN])
    w_uq = ext("w_uq", [depth, QL, 3072])
    w_ukv = ext("w_ukv", [depth, KVL, 4096])
    w_oa = ext("w_oa", [depth, 512, D])
    w_ob = ext("w_ob", [depth, D, D])
    w_out = ext("w_out", [depth, D, D])
    w_up = ext("w_up", [depth, D, 2 * DFF])
    w_down = ext("w_down", [depth, DFF, D])
    OUT = nc.dram_tensor("out", [nvc] + E4, F32, kind="ExternalOutput").ap()

    X1 = scr("X1", [nvc] + E4, F32)
    XM = scr("XM", [nvc] + E4, F32)
    QA = scr("QA", [nvc] + E4)
    KAB = scr("KAB", [nvc + 2] + E4)
    VAB = scr("VAB", [nvc + 2, TOK, 2048])
    QN = scr("QN", [nvc] + E4)
    QR = scr("QR", [nvc] + E4)
    KN = scr("KN", [nvc] + E4)
    KPEB = scr("KPEB", [nvc] + E4)
    KPE = KPEB[:, 0, 0:64, :]
    VB = scr("VB", [nvc, NHB, 128, TOK // 128, 128])
    OB = scr("OB", [nvc] + E4)
    OA = scr("OA", [nvc] + E4)
    H2B = scr("H2B", [nvc + 2] + E4)
    s_x = scr("s_x", E4, F32)
    s_pos = scr("s_pos", [128, TOK], I32)
    s_msk = scr("s_msk", [128, 4, 256], F32)
    s_qa = scr("s_qa", [12, 128, TOK])
    s_ka = scr("s_ka", [12, 128, TOK])
    s_va = scr("s_va", [TOK, WA])
    s_qn = scr("s_qn", [NHB, 128, TOK])
    s_qr = scr("s_qr", [NHB // 2, 128, TOK])
    s_kn = scr("s_kn", [NHB, 128, TOK])
    s_kpe = scr("s_kpe", [64, TOK])
    s_vb = scr("s_vb", [NHB, 128, TOK // 128, 128])
    s_kaw = scr("s_kaw", [12, 128, 2 * TOK])
    s_vaw = scr("s_vaw", [2 * TOK, WA])
    s_ob = scr("s_ob", [NHB, 128, TOK])
    s_oa = scr("s_oa", [4, 128, TOK])
    s_xm = scr("s_xm", E4, F32)
    s_h2 = scr("s_h2", E4)
    s_h2x = scr("s_h2x", [DC, 128, TOK + 2])
    s_xo = scr("s_xo", E4, F32)
    s_kb = scr("s_kb", [3] + E4)
    s_vb3 = scr("s_vb3", [3, TOK, 2048])
    s_hb = scr("s_hb", [3] + E4)

    def cp(q, out, in_, tag, **kw):
        P.dma(q, "cp_" + tag, out, in_, **kw)

    ctx.begin()
    dummy = ctx.sb("dummy_sb", [128, 16], BF16)
    P.dma("pool", "kdummy", dummy[:], cst_d, writes=["dummy"])
    z = ctx.sb("zeros_sb", [128, 2048], BF16)
    P.op("dve", lambda: nc.vector.memset(z[:], 0.0), writes=["z"])
    fps = ctx.psa("fence_ps", [128, 512], F32)
    for blk in (0, nvc + 1):
        for c in range(DC):
            P.dma("sp", "init0", KAB[blk, c], z[:, :], reads=["z"])
            P.dma("sp", "init1", H2B[blk, c], z[:, :], reads=["z"])
            P.dma("sp", "init2", VAB[blk, c * 128:(c + 1) * 128, :], z[:, :], reads=["z"])
    wb = {}
    for nm, t in (("w_in", w_in), ("w_uq", w_uq), ("w_ukv", w_ukv), ("w_oa", w_oa), ("w_ob", w_ob),
                  ("w_out", w_out), ("w_up", w_up), ("w_down", w_down)):
        tb = scr(nm + "_bf", list(t.shape), BF16)
        wb[nm] = tb
        rows = t.shape[1]
        for l_ in range(depth):
            for r0 in range(0, rows, 128):
                P.dma("pool", "cv_%s" % nm, tb[l_, r0:r0 + 128, :], t[l_, r0:r0 + 128, :])
    w_in, w_uq, w_ukv, w_oa, w_ob, w_out, w_up, w_down = (wb[k] for k in ("w_in", "w_uq", "w_ukv", "w_oa", "w_ob", "w_out", "w_up", "w_down"))
    P.hard_reset((fps[0:1, 0:1], "fence_ps", P.relay_w[:, 0:1]))

    inst = [0]

    def run(fn, tensors, pre, post, **kw):
        inst[0] += 1
        ctx.sfx = "_i%d" % inst[0]
        ctx.t = tensors
        ctx.begin()
        fz = ctx.psa("fence_ps", [128, 512], F32)
        pre()
        P.hard_reset((fz[0:1, 0:1], "fence_ps", P.relay_w[:, 0:1]))
        ctx.begin()
        ps, pskey = fn(ctx=ctx, **kw)
        P.fence((ps, pskey, P.relay_w[:, 0:1]))
        post()
        P.hard_reset((ps, pskey, P.relay_w[:, 0:1]))

    for l in range(depth):
        Xin = xT0 if l == 0 else X1
        last = (l == depth - 1)
        with nc.Fori(0, nvc) as vc:
            def pre_a():
                cp("sp", s_x, Xin[vc], "x")
                cp("aq", s_pos[0:64], pos[vc], "pos0")
                cp("aq", s_pos[64:128], pos[vc], "pos1")

            def post_a():
                cp("sp", QA[vc][0:12], s_qa, "qa")
                cp("aq", KAB[1:][vc][0:12], s_ka, "ka")
                cp("aq", VAB[1:][vc][:, 0:WA], s_va, "va")
                cp("sp", QN[vc], s_qn, "qn")
                cp("sp", QR[vc][0:NHB // 2], s_qr, "qr")
                cp("sp", KN[vc], s_kn, "kn")
                cp("sp", KPEB[vc][0, 0:64, :], s_kpe, "kpe")
                cp("sp", VB[vc], s_vb, "vb")
            run(build_phase_a, {
                "xT": s_x, "pos": s_pos, "gmix": gmix[l], "w_in": w_in[l], "nq": nq[l], "nkv": nkv[l],
                "w_uq": w_uq[l], "w_ukv": w_ukv[l], "cst": cst_d, "perm128": perm128_d, "perm64": perm64_d,
                "qaT": s_qa, "kaT": s_ka, "va": s_va, "QnT": s_qn, "QrT": s_qr, "KnT": s_kn, "kpeT": s_kpe, "Vb": s_vb},
                pre_a, post_a)
        with nc.Fori(0, nvc) as vc:
            def pre_b():
                cp("sp", s_qn, QN[vc], "qn")
                cp("sp", s_qr, QR[vc][0:NHB // 2], "qr")
                cp("sp", s_qa, QA[vc][0:12], "qa")
                cp("sp", s_kb[0], KAB[vc], "kb0", writes=["kb0"])
                cp("aq", s_kb[1], KAB[1:][vc], "kb1", writes=["kb1"])
                cp("aq", s_kb[2], KAB[2:][vc], "kb2", writes=["kb2"])
                cp("sp", s_vb3[0], VAB[vc], "vb0", writes=["vb0"])
                cp("aq", s_vb3[1], VAB[1:][vc], "vb1", writes=["vb1"])
                cp("aq", s_vb3[2], VAB[2:][vc], "vb2", writes=["vb2"])
                cp("sp", s_kaw[:, :, 0:1024], s_kb[0, 0:12, :, 1024:2048], "k0", reads=["kb0"])
                cp("aq", s_kaw[:, :, 1024:3072], s_kb[1, 0:12], "k1", reads=["kb1"])
                cp("aq", s_kaw[:, :, 3072:4096], s_kb[2, 0:12, :, 0:1024], "k2", reads=["kb2"])
                cp("sp", s_vaw[0:1024, :], s_vb3[0, 1024:2048, 0:WA], "v0", reads=["vb0"])
                cp("aq", s_vaw[1024:3072, :], s_vb3[1, :, 0:WA], "v1", reads=["vb1"])
                cp("aq", s_vaw[3072:4096, :], s_vb3[2, 0:1024, 0:WA], "v2", reads=["vb2"])
                cp("aq", s_msk, masks[vc], "msk")

            def post_b():
                cp("sp", OB[vc], s_ob, "ob")
                cp("sp", OA[vc][0:4], s_oa, "oa")
            run(build_phase_b, {
                "QnT": s_qn, "QrT": s_qr.rearrange("a (b p) t -> (a b) p t", b=2), "KnT_all": KN, "kpeT_all": KPE, "Vb_all": VB,
                "qaT": s_qa, "kaT_halo": s_kaw, "va_halo": s_vaw, "masks": s_msk, "cst": cst_d, "obT": s_ob, "oaT": s_oa},
                pre_b, post_b, ncores_b=nvc)

            def pre_c1():
                cp("sp", s_x, Xin[vc], "x")

            def post_c1():
                cp("sp", XM[vc], s_xm, "xm")
                cp("aq", H2B[1:][vc], s_h2, "h2")
            run(build_phase_c1, {
                "xT": s_x, "gmix": gmix[l], "gffn": gffn[l], "bgate": bgate[l], "w_in": w_in[l], "w_oa": w_oa[l],
                "w_ob": w_ob[l], "w_out": w_out[l], "oaT": s_oa, "obT": s_ob, "cst": cst_d, "xmT": s_xm, "h2T": s_h2},
                pre_c1, post_c1)
        with nc.Fori(0, nvc) as vc:
            def pre_c2():
                cp("sp", s_xm, XM[vc], "xm")
                cp("sp", s_hb[0], H2B[vc], "hb0", writes=["hb0"])
                cp("aq", s_hb[1], H2B[1:][vc], "hb1", writes=["hb1"])
                cp("aq", s_hb[2], H2B[2:][vc], "hb2", writes=["hb2"])
                cp("aq", s_h2x[:, :, 1:TOK + 1], s_hb[1], "h2a", reads=["hb1"])
                cp("sp", s_h2x[:, :, 0:1], s_hb[0, :, :, TOK - 1:TOK], "h2b", reads=["hb0"], allow_slow_non_contiguous=True)
                cp("sp", s_h2x[:, :, TOK + 1:TOK + 2], s_hb[2, :, :, 0:1], "h2c", reads=["hb2"], allow_slow_non_contiguous=True)

            def post_c2():
                cp("sp", (OUT[vc] if last else X1[vc]), s_xo, "xo")
            run(build_phase_c2, {
                "h2x": s_h2x, "xmT": s_xm, "w_up": w_up[l], "w_down": w_down[l],
                "conv_w": cw[l], "conv_b": cb[l], "gfin": gfin, "cst": cst_d, "xoT": s_xo},
                pre_c2, post_c2, final=last)
    return nc


def _fm2(v, nchunk):
    v = np.asarray(v, np.float32)
    return np.ascontiguousarray(v.reshape(v.shape[0], nchunk, 128).transpose(0, 2, 1))


def kernel(x, positions, norm_mix, w_in, b_gate, norm_q, w_uq, norm_kv, w_ukv,
           w_oa, w_ob, w_out, norm_ffn, w_up, conv_w, conv_b, w_down, norm_final):
    nvc = NCORES
    x = np.asarray(x, np.float32)
    pos = np.asarray(positions).reshape(S).astype(np.int32)
    cst, perm128, perm64 = _consts()
    xT = np.ascontiguousarray(x[0].reshape(nvc, TOK, DC, 128).transpose(0, 2, 3, 1))
    masks = np.zeros((nvc, 128, 4, 256), np.float32)
    for vc in range(nvc):
        masks[vc] = _dil_masks(vc, nvc)
    bg = np.asarray(b_gate, np.float32)
    bgate = np.ascontiguousarray(bg.reshape(DEPTH, 2, DC, 128).transpose(0, 3, 1, 2))
    cwv = np.asarray(conv_w, np.float32)
    cw = np.ascontiguousarray(cwv.reshape(DEPTH, 3, 2 * FC, 128).transpose(0, 3, 1, 2))
    in_map = {
        "xT": xT,
        "pos": np.ascontiguousarray(np.broadcast_to(pos.reshape(nvc, 1, TOK), (nvc, 64, TOK))).astype(np.int32),
        "masks": masks, "cst": cst, "perm128": perm128, "perm64": perm64,
        "gmix": _fm2(norm_mix, DC), "gffn": _fm2(norm_ffn, DC), "gfin": _fm(norm_final, DC),
        "bgate": bgate, "nq": _fm2(norm_q, 4), "nkv": _fm2(norm_kv, 4),
        "conv_w": cw, "conv_b": _fm2(conv_b, 2 * FC),
        "w_in": np.asarray(w_in, np.float32), "w_uq": np.asarray(w_uq, np.float32), "w_ukv": np.asarray(w_ukv, np.float32),
        "w_oa": np.asarray(w_oa, np.float32), "w_ob": np.asarray(w_ob, np.float32), "w_out": np.asarray(w_out, np.float32),
        "w_up": np.asarray(w_up, np.float32), "w_down": np.asarray(w_down, np.float32),
    }
    nc = build_fused(nvc, DEPTH)
    res = run_bass_kernel_spmd(nc, [in_map], core_ids=[0])
    o = np.asarray(res.results[0]["out"], np.float32)
    out = o.transpose(0, 3, 1, 2).reshape(1, S, D)
    return np.ascontiguousarray(out)
```

```python
import math
import numpy as np
import ml_dtypes
import concourse.bass as bass
import concourse.mybir as mybir
from concourse.bass_utils import run_bass_kernel_spmd

F32 = mybir.dt.float32
BF16 = mybir.dt.bfloat16
I32 = mybir.dt.int32
AF = mybir.ActivationFunctionType
ALU = mybir.AluOpType

NCORES = 8
S = 16384
TOK = S // NCORES
D = 2048
DC = D // 128
DEPTH = 2
WA = 1536
QL = 512
KVL = 512
KR = 64
NIN = 9792
O_QA, O_KA, O_VA = 0, 1536, 3072
O_QL = 4608
O_KVL = 5120
O_KR = 5632
O_G = 5696
NHB = 16
DFF = 5632
FC = DFF // 128
EPS = 1e-6
TWO_PI = 2.0 * math.pi


class Ctr:
    def __init__(self, nc, name):
        self.sem = nc.alloc_semaphore(name)
        self.n = 0
        self.name = name
        self.waiter = None


class Prog:
    CE = ("pe", "act", "dve")

    def __init__(self, nc):
        self.nc = nc
        self.eng = {"pe": nc.tensor, "act": nc.scalar, "dve": nc.vector,
                    "pool": nc.gpsimd, "sp": nc.sync}
        self.vc = {k: {} for k in self.eng}
        self.insts = {k: [] for k in self.CE}
        self.S = {}
        self.notes = {}
        self.lastw = {}
        self.readers = {}
        self.dma_ctrs = {}
        self.dma_vc = {}
        self.pe_relay = None
        self.nrelay = 0
        for p in self.CE:
            for c in ("pe", "act", "dve", "pool", "sp"):
                self.sem(p, c)
        self.free_ctrs = {"sp": [], "pool": []}
        self.nctr = 0
        for q, cnt in (("sp", 44), ("pool", 20)):
            for i in range(cnt):
                self.nctr += 1
                c = Ctr(nc, "d%s%d" % (q, self.nctr))
                c.uid = "d%d" % self.nctr
                c.q = q
                self.free_ctrs[q].append(c)
            self.free_ctrs[q].reverse()
        self.relay_buf = nc.alloc_sbuf_tensor("relay_buf", [128, 1008], F32)
        self.op("dve", lambda: nc.vector.memset(self.relay_buf[:], 0.0), writes=["relay_buf"])
        self.relay_w = nc.alloc_sbuf_tensor("relay_w", [128, 8], BF16)
        self.op("dve", lambda: nc.vector.memset(self.relay_w[:], 0.0), writes=["relay_w"])

    def sem(self, p, c):
        if (p, c) not in self.S:
            self.S[(p, c)] = Ctr(self.nc, "s_%s_%s" % (p, c))
            self.notes[(p, c)] = []
        return self.S[(p, c)]

    def dctr(self, name, q="sp"):
        if name not in self.dma_ctrs:
            assert self.free_ctrs[q], "out of DMA counters"
            c = self.free_ctrs[q].pop()
            self.dma_ctrs[name] = c
        c = self.dma_ctrs[name]
        assert c.q == q, (name, c.q, q)
        return c

    @staticmethod
    def _merge(a, b):
        for k, v in b.items():
            if a.get(k, 0) < v:
                a[k] = v

    def _find_direct(self, e, p, k):
        if (p, e) not in self.S:
            notes = []
        else:
            notes = self.notes[(p, e)]
        lo, hi = 0, len(notes)
        while lo < hi:
            mid = (lo + hi) // 2
            if notes[mid][0] >= k:
                hi = mid
            else:
                lo = mid + 1
        if lo < len(notes):
            return notes[lo]
        start = max(k, (notes[-1][0] + 1) if notes else 1)
        lst = self.insts[p]
        for j in range(start, min(len(lst), start + 64) + 1):
            if lst[j - 1][1] is None:
                return (j, None)
        return None

    def _first_knower(self, q, p, k):
        lst = self.insts[q]
        lo, hi = 0, len(lst)
        while lo < hi:
            mid = (lo + hi) // 2
            if lst[mid][2].get(p, 0) >= k:
                hi = mid
            else:
                lo = mid + 1
        return (lo + 1) if lo < len(lst) else None

    def _need_eng(self, e, p, k):
        if self.vc[e].get(p, 0) >= k:
            return
        src = p
        d = self._find_direct(e, p, k)
        if d is None:
            for q in self.CE:
                if q == p or q == e:
                    continue
                j = self._first_knower(q, p, k)
                if j is not None:
                    d2 = self._find_direct(e, q, j)
                    if d2 is not None:
                        src, d = q, d2
                        break
        if d is None:
            d = (self._relay(p), None)
        idx, val = d
        ctr = self.sem(src, e)
        if val is None:
            rec = self.insts[src][idx - 1]
            rec[0].then_inc(ctr.sem, 1)
            ctr.n += 1
            rec[1] = e
            val = ctr.n
            self.notes[(src, e)].append((idx, val))
        assert ctr.waiter in (None, e)
        ctr.waiter = e
        self.eng[e].wait_ge(ctr.sem, val)
        self._merge(self.vc[e], self.insts[src][idx - 1][2])

    def _relay(self, p):
        nc = self.nc
        self.nrelay += 1
        r = self.nrelay
        pb = 32 * (r % 4)
        col = 8 + (r // 4) % 1000
        if p == "dve":
            ins = nc.vector.memset(self.relay_buf[pb:pb + 1, col:col + 1], 0.0)
        elif p == "act":
            if not getattr(self, "relay_ready", False):
                self._need_eng("act", "dve", 1)
            ins = nc.scalar.copy(out=self.relay_buf[pb:pb + 1, col:col + 1], in_=self.relay_buf[0:1, 4:5])
        else:
            assert self.pe_relay is not None, "PE relay needed but no PSUM scratch configured"
            ps, lhsT, rhs = self.pe_relay
            ins = nc.tensor.matmul(ps, lhsT=lhsT, rhs=rhs, start=True, stop=True)
        vcd = dict(self.vc[p])
        self.insts[p].append([ins, None, vcd])
        vcd[p] = len(self.insts[p])
        return len(self.insts[p])

    def _need_dma(self, e, c, v):
        if self.vc[e].get(c.uid, 0) >= v:
            return
        assert c.waiter in (None, e), "DMA ctr %s waited by %s and %s" % (c.uid, c.waiter, e)
        c.waiter = e
        self.eng[e].wait_ge(c.sem, v)
        self._merge(self.vc[e], self.dma_vc[(c.uid, v)])
        self.vc[e][c.uid] = v

    def _need(self, e, dep):
        if dep is None:
            return
        if dep[0] == "eng":
            self._need_eng(e, dep[1], dep[2])
        else:
            self._need_dma(e, dep[1], dep[2])

    def _deps(self, e, reads, writes, pe_accum=False, same_dma=None):
        deps = []
        for k in reads:
            deps.append(self.lastw.get(k))
        for k in writes:
            lw = self.lastw.get(k)
            if pe_accum and lw is not None and lw[0] == "eng" and lw[1] == "pe":
                pass
            elif same_dma is not None and lw is not None and lw[0] == "dma" and lw[1] is self.dma_ctrs.get(same_dma):
                pass
            else:
                deps.append(lw)
            deps.extend(self.readers.get(k, {}).values())
        for d in deps:
            if d is not None and d[0] == "eng":
                self._need(e, d)
        for d in deps:
            if d is not None and d[0] != "eng":
                self._need(e, d)

    def _record(self, dep, reads, writes):
        src = (dep[0], dep[1] if dep[0] == "eng" else dep[1].uid)
        for k in reads:
            r = self.readers.setdefault(k, {})
            if src not in r or r[src][2] < dep[2]:
                r[src] = dep
        for k in writes:
            self.lastw[k] = dep
            self.readers[k] = {}

    def op(self, e, fn, reads=(), writes=()):
        self._deps(e, reads, writes, pe_accum=(e == "pe"))
        ins = fn()
        vcd = dict(self.vc[e])
        self.insts[e].append([ins, None, vcd])
        idx = len(self.insts[e])
        vcd[e] = idx
        dep = ("eng", e, idx)
        self._record(dep, reads, writes)
        return dep

    def dma(self, q, cname, out, in_, reads=(), writes=(), **kw):
        eq = {"aq": "act", "vq": "dve"}.get(q, q)
        self._deps(eq, reads, writes, same_dma=cname)
        c = self.dctr(cname, "sp" if q in ("aq", "vq") else q)
        ins = self.eng[eq].dma_start(out=out, in_=in_, **kw)
        ins.then_inc(c.sem, 16)
        c.n += 16
        self.dma_vc[(c.uid, c.n)] = dict(self.vc[eq])
        dep = ("dma", c, c.n)
        self._record(dep, reads, writes)
        return dep

    def finish(self, prefixes):
        for name, c in self.dma_ctrs.items():
            if any(name.startswith(p) for p in prefixes):
                e = c.waiter or "sp"
                if c.n > 0:
                    self._need_dma(e, c, c.n)

    def hard_reset(self, pe_scratch):
        nc = self.nc
        self.fence(pe_scratch)
        nc.all_engine_barrier()
        allc = list(self.S.values()) + self.free_ctrs["sp"] + list(self.dma_ctrs.values())
        for c in allc:
            if getattr(c, "q", None) == "pool":
                continue
            nc.gpsimd.sem_clear(c.sem)
            c.n = 0
            c.waiter = None
        nc.all_engine_barrier()
        self.notes = {k: [] for k in self.notes}
        self.insts = {k: [] for k in self.CE}
        self.vc = {k: {} for k in self.eng}
        self.dma_vc = {}
        self.relay_ready = True

    def fence(self, pe_scratch):
        nc = self.nc
        for name, c in self.dma_ctrs.items():
            if c.n > 0:
                self._need_dma(c.waiter or "dve", c, c.n)
        ps, pskey, w = pe_scratch
        self.op("pe", lambda: nc.tensor.matmul(ps, lhsT=w, rhs=w, start=True, stop=True), reads=["relay_w"], writes=[pskey])
        self._need_eng("dve", "pe", len(self.insts["pe"]))
        ia = self._relay("act")
        self._need_eng("dve", "act", ia)
        for e in ("pe", "act", "sp", "pool"):
            i = self._relay("dve")
            self._need_eng(e, "dve", i)
        self.lastw = {}
        self.readers = {}
        for c in self.dma_ctrs.values():
            c.waiter = None
            self.free_ctrs[c.q].append(c)
        self.dma_ctrs = {}


class Rot:
    def __init__(self, name, bufs):
        self.name = name
        self.bufs = bufs
        self.i = 0

    def next(self):
        j = self.i % len(self.bufs)
        self.i += 1
        return self.bufs[j], (self.name, j)


def _consts():
    p = np.arange(128)
    invf128 = (1.0 / (np.float32(10000.0) ** (np.arange(0, 128, 2, dtype=np.float32) / np.float32(128)))).astype(np.float32)
    invf64 = (1.0 / (np.float32(10000.0) ** (np.arange(0, 64, 2, dtype=np.float32) / np.float32(64)))).astype(np.float32)
    cst = np.zeros((128, 16), np.float32)
    sgn128 = np.where(p < 64, -1.0, 1.0).astype(np.float32)
    sgn64 = np.where((p % 64) < 32, -1.0, 1.0).astype(np.float32)
    cst[:, 0] = invf128[p % 64]
    cst[:, 1] = invf64[p % 32]
    cst[:, 2] = sgn128
    cst[:, 3] = sgn64
    cst[:, 4] = np.float32(math.pi / 2)
    cst[:, 5] = np.float32(EPS)
    perm128 = np.zeros((128, 128), np.float32)
    perm128[(p + 64) % 128, p] = 1.0
    perm64 = np.zeros((128, 128), np.float32)
    perm64[(p // 64) * 64 + ((p % 64) + 32) % 64, p] = 1.0
    return cst, perm128, perm64


class _Stop(Exception):
    pass


def build_phase_a(stop=None, ctx=None):
    if ctx is None:
        nc = bass.Bass("TRN2", target_bir_lowering=False)
        P = Prog(nc)
        sfx = ""
    else:
        nc, P, sfx = ctx.nc, ctx.P, ctx.sfx
    TT = 1024
    NTT = TOK // TT

    def din(name, shape, dt=F32):
        if ctx is not None:
            ap = ctx.t[name]
            assert list(ap.shape) == list(shape), (name, ap.shape, shape)
            return ap
        return nc.dram_tensor(name, shape, dt, kind="ExternalInput").ap()

    def dout(name, shape, dt=BF16):
        if ctx is not None:
            ap = ctx.t[name]
            assert list(ap.shape) == list(shape), (name, ap.shape, shape)
            return ap
        return nc.dram_tensor(name, shape, dt, kind="ExternalOutput").ap()

    xT = din("xT", [DC, 128, TOK])
    pos = din("pos", [128, TOK], I32)
    gmix = din("gmix", [128, DC])
    w_in = din("w_in", [D, NIN])
    nq = din("nq", [128, 4])
    nkv = din("nkv", [128, 4])
    w_uq = din("w_uq", [QL, 3072])
    w_ukv = din("w_ukv", [KVL, 4096])
    cst_d = din("cst", [128, 16])
    perm128_d = din("perm128", [128, 128])
    perm64_d = din("perm64", [128, 128])

    qaT = dout("qaT", [12, 128, TOK])
    kaT = dout("kaT", [12, 128, TOK])
    va = dout("va", [TOK, WA])
    QnT = dout("QnT", [NHB, 128, TOK])
    QrT = dout("QrT", [NHB // 2, 128, TOK])
    KnT = dout("KnT", [NHB, 128, TOK])
    kpeT = dout("kpeT", [64, TOK])
    Vb = dout("Vb", [NHB, 128, TOK // 128, 128])

    if ctx is None:
        sb = lambda name, shape, dt: nc.alloc_sbuf_tensor(name, shape, dt)
        psa = lambda name, shape, dt: nc.alloc_psum_tensor(name, shape, dt)
    else:
        sb, psa = ctx.sb, ctx.psa
    cst = sb("cst_sb", [128, 16], F32)
    cstA = sb("cstA_sb", [128, 16], F32)
    gm = sb("gm_sb", [128, DC], F32)
    nq_sb = sb("nq_sb", [128, 4], F32)
    nkv_sb = sb("nkv_sb", [128, 4], F32)
    perm128 = sb("perm128_sb", [128, 128], BF16)
    perm64 = sb("perm64_sb", [128, 128], BF16)
    ones = sb("ones_sb", [128, 128], F32)
    hT = sb("hT", [128, DC, TT], BF16)
    wbufs = Rot("wbuf", [sb(f"wbuf{i}", [128, DC, 512], BF16) for i in range(3)])
    xst = Rot("xst", [sb(f"xst{i}", [128, TT], F32) for i in range(2)])
    xsd = Rot("xsd", [sb(f"xsd{i}", [128, TT], F32) for i in range(2)])
    sqb = Rot("sqb", [sb(f"sqb{i}", [128, TT], F32) for i in range(2)])
    rstd = sb("rstd", [128, TT], F32)
    rtmp = sb("rtmp", [128, TT], F32)
    lat = sb("lat", [128, 4, TT], F32)
    cqn = sb("cqn", [128, 4, TT], BF16)
    ckvn = sb("ckvn", [128, 4, TT], BF16)
    posf = sb("posf", [128, TT], F32)
    posi = sb("posi", [128, TT], I32)
    ang = sb("ang", [128, TT], F32)
    qf = sb("qf", [128, TT], F32)
    cos128 = sb("cos128", [128, TT], F32)
    sin128 = sb("sin128", [128, TT], F32)
    cos64 = sb("cos64", [128, TT], F32)
    sin64 = sb("sin64", [128, TT], F32)
    xb = Rot("xb", [sb(f"xb{i}", [128, 512], BF16) for i in range(2)])
    t1 = Rot("t1", [sb(f"t1_{i}", [128, 512], F32) for i in range(2)])
    t2 = Rot("t2", [sb(f"t2_{i}", [128, 512], F32) for i in range(2)])
    ost = Rot("ost", [sb(f"ost{i}", [128, 512], BF16) for i in range(4)])
    ostA = Rot("ostA", [sb(f"ostA{i}", [128, 512], BF16) for i in range(4)])
    psum = Rot("ps", [psa(f"ps{i}", [128, 512], F32) for i in range(6)])
    psw = Rot("psw", [psa(f"psw{i}", [128, 512], F32) for i in range(2)])

    P.dma("sp", "k0", cst[:], cst_d, writes=["cst"])
    P.dma("sp", "k0a", cstA[:], cst_d, writes=["cstA"])
    P.dma("sp", "k1", gm[:], gmix, writes=["gm"])
    P.dma("sp", "k2", nq_sb[:], nq, writes=["nq"])
    P.dma("sp", "k3", nkv_sb[:], nkv, writes=["nkv"])
    if ctx is None:
        dummy = sb("dummy_sb", [128, 16], BF16)
        P.dma("pool", "kdummy", dummy[:], cst_d, writes=["dummy"])
    pst = sb("perm_stage", [128, 2, 128], F32)
    P.dma("sp", "k4", pst[:, 0, :], perm128_d, writes=["pst0"])
    P.dma("sp", "k5", pst[:, 1, :], perm64_d, writes=["pst1"])
    P.op("dve", lambda: nc.vector.tensor_copy(out=perm128[:], in_=pst[:, 0, :]), reads=["pst0"], writes=["perm128"])
    P.op("dve", lambda: nc.vector.tensor_copy(out=perm64[:], in_=pst[:, 1, :]), reads=["pst1"], writes=["perm64"])
    P.op("dve", lambda: nc.vector.memset(ones[:], 1.0), writes=["ones"])

    def load_w(dst_ap, src_ap, key):
        return P.dma("pool" if ctx is None else "sp", "w_" + str(key[1]), dst_ap, src_ap, writes=[key])

    def rope_tables(tt):
        P.dma("sp", "pos", posi[:], pos[:, tt * TT:(tt + 1) * TT], writes=["posi"])
        P.op("dve", lambda: nc.vector.tensor_copy(out=posf[:], in_=posi[:]), reads=["posi"], writes=["posf"])
        C1 = 6.28125
        C2 = TWO_PI - C1
        for (ci, cos_t, sin_t, cname, sname) in ((0, cos128, sin128, "cos128", "sin128"), (1, cos64, sin64, "cos64", "sin64")):
            P.op("dve", lambda: nc.vector.tensor_scalar(out=ang[:], in0=posf[:], scalar1=cst[:, ci:ci + 1], scalar2=None,
                                                        op0=ALU.mult), reads=["posf", "cst"], writes=["ang"])
            P.op("dve", lambda: nc.vector.tensor_scalar(out=qf[:], in0=ang[:], scalar1=1.0 / TWO_PI, scalar2=None,
                                                        op0=ALU.mult), reads=["ang"], writes=["qf"])
            P.op("dve", lambda: nc.vector.tensor_copy(out=posi[:], in_=qf[:]), reads=["qf"], writes=["posi"])
            P.op("dve", lambda: nc.vector.tensor_copy(out=qf[:], in_=posi[:]), reads=["posi"], writes=["qf"])
            P.op("dve", lambda: nc.vector.scalar_tensor_tensor(out=ang[:], in0=qf[:], scalar=-C1, in1=ang[:],
                                                               op0=ALU.mult, op1=ALU.add), reads=["qf", "ang"], writes=["ang"])
            P.op("dve", lambda: nc.vector.scalar_tensor_tensor(out=ang[:], in0=qf[:], scalar=-C2, in1=ang[:],
                                                               op0=ALU.mult, op1=ALU.add), reads=["qf", "ang"], writes=["ang"])
            P.op("dve", lambda: nc.vector.tensor_scalar(out=ang[:], in0=ang[:], scalar1=math.pi, scalar2=-math.pi,
                                                        op0=ALU.min, op1=ALU.max), reads=["ang"], writes=["ang"])
            P.op("act", lambda: nc.scalar.activation(out=sin_t[:], in_=ang[:], func=AF.Sin, scale=cstA[:, 2 + ci:3 + ci]),
                 reads=["ang", "cstA"], writes=[sname])
            P.op("act", lambda: nc.scalar.activation(out=qf[:], in_=ang[:], func=AF.Abs), reads=["ang"], writes=["qf"])
            P.op("act", lambda: nc.scalar.activation(out=cos_t[:], in_=qf[:], func=AF.Sin, scale=-1.0, bias=cstA[:, 4:5]),
                 reads=["qf", "cstA"], writes=[cname])

    def sumsq_rstd(src_list, nchunks):
        pw = []
        for hf in range(TT // 512):
            pw.append(psw.next())
        for c in range(nchunks):
            ap, key = src_list(c)
            sq, sqk = sqb.next()
            P.op("act", lambda: nc.scalar.activation(out=sq[:], in_=ap, func=AF.Square), reads=[key], writes=[sqk])
            for hf in range(TT // 512):
                pt, pk = pw[hf]
                P.op("pe", lambda: nc.tensor.matmul(pt[:], lhsT=ones[:], rhs=sq[:, hf * 512:(hf + 1) * 512],
                                                    start=(c == 0), stop=(c == nchunks - 1)),
                     reads=["ones", sqk], writes=[pk])
        for hf in range(TT // 512):
            pt, pk = pw[hf]
            P.op("act", lambda: nc.scalar.activation(out=rtmp[:, hf * 512:(hf + 1) * 512], in_=pt[:], func=AF.Sqrt,
                                                     scale=1.0 / (128 * nchunks), bias=cstA[:, 5:6]),
                 reads=[pk, "cstA"], writes=[("rtmp", hf)])
            P.op("dve", lambda: nc.vector.reciprocal(out=rstd[:, hf * 512:(hf + 1) * 512], in_=rtmp[:, hf * 512:(hf + 1) * 512]),
                 reads=[("rtmp", hf)], writes=[("rstd", hf)])

    def rope_evac(pt, pk, M, cos_t, sin_t, cname, sname, perm, permk, hf, dst_ap, dstk):
        xbt, xbk = xb.next()
        P.op("act", lambda: nc.scalar.copy(out=xbt[:M, :], in_=pt[:M, :]), reads=[pk], writes=[xbk])
        p2, p2k = psum.next()
        P.op("pe", lambda: nc.tensor.matmul(p2[:M, :], lhsT=perm[:M, :M], rhs=xbt[:M, :], start=True, stop=True),
             reads=[permk, xbk], writes=[p2k])
        a, ak = t1.next()
        b, bk = t2.next()
        cs = slice(hf * 512, (hf + 1) * 512)
        P.op("dve", lambda: nc.vector.tensor_tensor(out=a[:M, :], in0=pt[:M, :], in1=cos_t[:M, cs], op=ALU.mult),
             reads=[pk, cname], writes=[ak])
        P.op("dve", lambda: nc.vector.tensor_tensor(out=b[:M, :], in0=p2[:M, :], in1=sin_t[:M, cs], op=ALU.mult),
             reads=[p2k, sname], writes=[bk])
        P.op("dve", lambda: nc.vector.tensor_tensor(out=dst_ap, in0=a[:M, :], in1=b[:M, :], op=ALU.add),
             reads=[ak, bk], writes=[dstk])

    w_in_v = w_in.rearrange("(kc k) n -> k kc n", k=128)
    w_uq_v = w_uq.rearrange("(kc k) n -> k kc n", k=128)
    w_ukv_v = w_ukv.rearrange("(kc k) n -> k kc n", k=128)

    def ck(name):
        if stop == name:
            raise _Stop()

    try:
        for tt in range(NTT):
          tsl = slice(tt * TT, (tt + 1) * TT)
          rope_tables(tt)
          ck('rope')
          wq_t, wq_k = wbufs.next()
          load_w(wq_t[:], w_in_v[:, :, O_QL:O_QL + 512], wq_k)

          def xsrc(c):
              t, k = xst.next()
              P.dma("sp", "x_" + str(k[1]), t[:], xT[c, :, tsl], writes=[k])
              return t[:], k
          sumsq_rstd(xsrc, DC)
          for c in range(DC):
              t, k = xsd.next()
              P.dma("sp", "xd_" + str(k[1]), t[:], xT[c, :, tsl], writes=[k])
              P.op("dve", lambda: nc.vector.scalar_tensor_tensor(out=hT[:, c, :], in0=t[:], scalar=gm[:, c:c + 1], in1=rstd[:],
                                                                 op0=ALU.mult, op1=ALU.mult),
                   reads=[k, "gm", ("rstd", 0), ("rstd", 1)], writes=[("hT", c)])
          ck('norm')
          hkeys = [("hT", c) for c in range(DC)]

          def proj_fm(wt, wk, col0, M, hf, kch=DC, rhs=None, rkeys=None):
              pt, pk = psum.next()
              for kc in range(kch):
                  r = (hT if rhs is None else rhs)
                  P.op("pe", lambda: nc.tensor.matmul(pt[:M, :], lhsT=wt[:, kc, col0:col0 + M], rhs=r[:, kc, hf * 512:(hf + 1) * 512],
                                                      start=(kc == 0), stop=(kc == kch - 1)),
                       reads=[wk] + (hkeys if rkeys is None else rkeys), writes=[pk])
              return pt, pk

          for li, (o_l, nrm_sb, nrmk, dstn, dstk) in enumerate(((O_QL, nq_sb, "nq", cqn, "cqn"), (O_KVL, nkv_sb, "nkv", ckvn, "ckvn"))):
              if li == 0:
                  wt, wk = wq_t, wq_k
              else:
                  wt, wk = wbufs.next()
                  load_w(wt[:], w_in_v[:, :, o_l:o_l + 512], wk)
              for sub in range(4):
                  for hf in range(TT // 512):
                      pt, pk = proj_fm(wt, wk, sub * 128, 128, hf)
                      P.op("act", lambda: nc.scalar.copy(out=lat[:, sub, hf * 512:(hf + 1) * 512], in_=pt[:]),
                           reads=[pk], writes=[("lat", sub)])
              sumsq_rstd(lambda c: (lat[:, c, :], ("lat", c)), 4)
              for sub in range(4):
                  P.op("dve", lambda: nc.vector.scalar_tensor_tensor(out=dstn[:, sub, :], in0=lat[:, sub, :], scalar=nrm_sb[:, sub:sub + 1],
                                                                     in1=rstd[:], op0=ALU.mult, op1=ALU.mult),
                       reads=[("lat", sub), nrmk, ("rstd", 0), ("rstd", 1)], writes=[(dstk, sub)])
          ck('lat')
          wt, wk = wbufs.next()
          load_w(wt[:, :, 0:64], w_in_v[:, :, O_KR:O_KR + 64], wk)
          for hf in range(TT // 512):
              pt, pk = proj_fm(wt, wk, 0, 64, hf)
              o, ok = ost.next()
              rope_evac(pt, pk, 64, cos64, sin64, "cos64", "sin64", perm64, "perm64", hf, o[:64, :], ok)
              P.dma("aq" if ctx is not None else "sp", "o_" + str(ok[1]), kpeT[:, tt * TT + hf * 512: tt * TT + (hf + 1) * 512], o[:64, :], reads=[ok])

          ck('kpe')
          for (o_c, dst) in ((O_QA, qaT), (O_KA, kaT)):
              for blk in range(3):
                  wt, wk = wbufs.next()
                  load_w(wt[:], w_in_v[:, :, o_c + blk * 512:o_c + (blk + 1) * 512], wk)
                  for sub in range(4):
                      head = blk * 4 + sub
                      for hf in range(TT // 512):
                          pt, pk = proj_fm(wt, wk, sub * 128, 128, hf)
                          o, ok = ost.next()
                          import os
                          if os.environ.get("DBG") == "norope":
                              P.op("act", lambda: nc.scalar.copy(out=o[:], in_=pt[:]), reads=[pk], writes=[ok])
                          else:
                              if os.environ.get("DBG") == "perm64":
                                  rope_evac(pt, pk, 128, cos128, sin128, "cos128", "sin128", perm64, "perm64", hf, o[:], ok)
                              elif os.environ.get("DBG") == "cos64":
                                  rope_evac(pt, pk, 128, cos64, sin64, "cos64", "sin64", perm128, "perm128", hf, o[:], ok)
                              else:
                                  rope_evac(pt, pk, 128, cos128, sin128, "cos128", "sin128", perm128, "perm128", hf, o[:], ok)
                          if os.environ.get("DBG") != "nodma":
                              P.dma("aq" if ctx is not None else "sp", "o_" + str(ok[1]), dst[head, :, tt * TT + hf * 512: tt * TT + (hf + 1) * 512], o[:], reads=[ok])
                          ck('qa_%d_%d_%d' % (o_c, head, hf))
          ck('qaka')
          for blk in range(3):
              wt, wk = wbufs.next()
              load_w(wt[:], w_in_v[:, :, O_VA + blk * 512:O_VA + (blk + 1) * 512], wk)
              for tb in range(TT // 128):
                  pt, pk = psum.next()
                  for kc in range(DC):
                      P.op("pe", lambda: nc.tensor.matmul(pt[:], lhsT=hT[:, kc, tb * 128:(tb + 1) * 128], rhs=wt[:, kc, :],
                                                          start=(kc == 0), stop=(kc == DC - 1)),
                           reads=[wk] + hkeys, writes=[pk])
                  o, ok = ostA.next()
                  P.op("act", lambda: nc.scalar.copy(out=o[:], in_=pt[:]), reads=[pk], writes=[ok])
                  r0 = tt * TT + tb * 128
                  P.dma("aq" if ctx is not None else "sp", "oA_" + str(ok[1]), va[r0:r0 + 128, blk * 512:(blk + 1) * 512], o[:], reads=[ok])

          ck('va')
          cqk = [("cqn", c) for c in range(4)]
          for half in range(2):
              wt, wk = wbufs.next()
              wv = wt[:].rearrange("p a b -> p (a b)")[:, 0:4 * 1536].rearrange("p (a b) -> p a b", b=1536)
              src = w_uq_v[:, :, half * 1536:(half + 1) * 1536].rearrange("p a (h d) -> p a h d", d=192)
              for a_ in range(4):
                  load_w(wv[:, a_, 0:1024].rearrange("p (h d) -> p h d", d=128), src[:, a_, :, 0:128], wk)
                  load_w(wv[:, a_, 1024:1536].rearrange("p (h d) -> p h d", d=64), src[:, a_, :, 128:192], wk)
              for hl in range(8):
                  head = half * 8 + hl
                  for hf in range(TT // 512):
                      pt, pk = psum.next()
                      for kc in range(4):
                          P.op("pe", lambda: nc.tensor.matmul(pt[:], lhsT=wv[:, kc, hl * 128:(hl + 1) * 128], rhs=cqn[:, kc, hf * 512:(hf + 1) * 512],
                                                              start=(kc == 0), stop=(kc == 3)),
                               reads=[wk] + cqk, writes=[pk])
                      o, ok = ostA.next()
                      P.op("act", lambda: nc.scalar.copy(out=o[:], in_=pt[:]), reads=[pk], writes=[ok])
                      P.dma("aq" if ctx is not None else "sp", "oA_" + str(ok[1]), QnT[head, :, tt * TT + hf * 512: tt * TT + (hf + 1) * 512], o[:], reads=[ok])
              for pr in range(4):
                  pair = half * 4 + pr
                  for hf in range(TT // 512):
                      pt, pk = psum.next()
                      for kc in range(4):
                          P.op("pe", lambda: nc.tensor.matmul(pt[:], lhsT=wv[:, kc, 1024 + pr * 128:1024 + (pr + 1) * 128], rhs=cqn[:, kc, hf * 512:(hf + 1) * 512],
                                                              start=(kc == 0), stop=(kc == 3)),
                               reads=[wk] + cqk, writes=[pk])
                      o, ok = ost.next()
                      rope_evac(pt, pk, 128, cos64, sin64, "cos64", "sin64", perm64, "perm64", hf, o[:], ok)
                      P.dma("aq" if ctx is not None else "sp", "o_" + str(ok[1]), QrT[pair, :, tt * TT + hf * 512: tt * TT + (hf + 1) * 512], o[:], reads=[ok])

          ck('qup')
          ckk = [("ckvn", c) for c in range(4)]
          for half in range(2):
              wt, wk = wbufs.next()
              wv = wt[:].rearrange("p a b -> p (a b)").rearrange("p (a b) -> p a b", b=2048)
              src = w_ukv_v[:, :, half * 2048:(half + 1) * 2048].rearrange("p a (h d) -> p a h d", d=256)
              for a_ in range(4):
                  load_w(wv[:, a_, 0:1024].rearrange("p (h d) -> p h d", d=128), src[:, a_, :, 0:128], wk)
                  load_w(wv[:, a_, 1024:2048].rearrange("p (h d) -> p h d", d=128), src[:, a_, :, 128:256], wk)
              for hl in range(8):
                  head = half * 8 + hl
                  for hf in range(TT // 512):
                      pt, pk = psum.next()
                      for kc in range(4):
                          P.op("pe", lambda: nc.tensor.matmul(pt[:], lhsT=wv[:, kc, hl * 128:(hl + 1) * 128], rhs=ckvn[:, kc, hf * 512:(hf + 1) * 512],
                                                              start=(kc == 0), stop=(kc == 3)),
                               reads=[wk] + ckk, writes=[pk])
                      o, ok = ostA.next()
                      P.op("act", lambda: nc.scalar.copy(out=o[:], in_=pt[:]), reads=[pk], writes=[ok])
                      P.dma("aq" if ctx is not None else "sp", "oA_" + str(ok[1]), KnT[head, :, tt * TT + hf * 512: tt * TT + (hf + 1) * 512], o[:], reads=[ok])
              for hq in range(2):
                  for tb in range(TT // 128):
                      pt, pk = psum.next()
                      for kc in range(4):
                          P.op("pe", lambda: nc.tensor.matmul(pt[:],
                                                              lhsT=ckvn[:, kc, tb * 128:(tb + 1) * 128],
                                                              rhs=wv[:, kc, 1024 + hq * 512:1024 + (hq + 1) * 512],
                                                              start=(kc == 0), stop=(kc == 3)),
                               reads=[wk] + ckk, writes=[pk])
                      o, ok = ostA.next()
                      P.op("act", lambda: nc.scalar.copy(out=o[:], in_=pt[:]), reads=[pk], writes=[ok])
                      h0 = half * 8 + hq * 4
                      j = tt * (TT // 128) + tb
                      P.dma("aq" if ctx is not None else "sp", "oA_" + str(ok[1]), Vb[h0:h0 + 4, :, j, :].rearrange("h p d -> p h d"),
                            o[:].rearrange("p (h d) -> p h d", d=128), reads=[ok])

    except _Stop:
        pass
    if ctx is not None:
        return (psw.bufs[0][0:1, 0:1], ("psw", 0))
    P.finish(["o_", "oA_"])
    return nc


def _fm(v, nchunk):
    return np.ascontiguousarray(np.asarray(v, np.float32).reshape(nchunk, 128).T)


def run_phase_a(inputs, layer, xT_cores):
    cst, perm128, perm64 = _consts()
    nc = build_phase_a()
    pos = np.asarray(inputs["positions"]).reshape(S)
    in_maps = []
    for c in range(NCORES):
        in_maps.append({
            "xT": xT_cores[c],
            "pos": np.ascontiguousarray(np.broadcast_to(pos[c * TOK:(c + 1) * TOK][None, :], (128, TOK))).astype(np.int32),
            "gmix": _fm(inputs["norm_mix"][layer], DC),
            "w_in": np.asarray(inputs["w_in"][layer]),
            "nq": _fm(inputs["norm_q"][layer], 4),
            "nkv": _fm(inputs["norm_kv"][layer], 4),
            "w_uq": np.asarray(inputs["w_uq"][layer]),
            "w_ukv": np.asarray(inputs["w_ukv"][layer]),
            "cst": cst, "perm128": perm128, "perm64": perm64,
        })
    res = run_bass_kernel_spmd(nc, in_maps, core_ids=list(range(NCORES)))
    return res.results


DIL = (1, 4, 16)


def _dil_masks(core, ncores=NCORES):
    p = np.arange(128)[:, None]
    f = np.arange(128)[None, :]
    A = (p >= f).astype(np.float32)
    B = (p <= f).astype(np.float32)
    m = np.zeros((128, 4, 256), np.float32)
    for v in range(4):
        a = A.copy()
        b = B.copy()
        if (v & 1) and core == 0:
            a[:64, :] = 0.0
        if (v & 2) and core == ncores - 1:
            b[64:, :] = 0.0
        m[:, v, 0:128] = a
        m[:, v, 128:256] = b
    return m


def build_phase_b(stop=None, nheads=NHB, do_dil=True, ctx=None, ncores_b=NCORES):
    if ctx is None:
        nc = bass.Bass("TRN2", target_bir_lowering=False)
        P = Prog(nc)
        sfx = ""
    else:
        nc, P, sfx = ctx.nc, ctx.P, ctx.sfx

    def din(name, shape, dt=BF16):
        if ctx is not None:
            ap = ctx.t[name]
            assert list(ap.shape) == list(shape), (name, ap.shape, shape)
            return ap
        return nc.dram_tensor(name, shape, dt, kind="ExternalInput").ap()

    def dout(name, shape, dt=BF16):
        if ctx is not None:
            ap = ctx.t[name]
            assert list(ap.shape) == list(shape), (name, ap.shape, shape)
            return ap
        return nc.dram_tensor(name, shape, dt, kind="ExternalOutput").ap()

    QnT = din("QnT", [NHB, 128, TOK])
    QrT = din("QrT", [NHB, 64, TOK])
    KnT = din("KnT_all", [ncores_b, NHB, 128, TOK])
    kpeT = din("kpeT_all", [ncores_b, 64, TOK])
    Vb = din("Vb_all", [ncores_b, NHB, 128, TOK // 128, 128])
    qaT = din("qaT", [12, 128, TOK])
    kaH = din("kaT_halo", [12, 128, 2 * TOK])
    vaH = din("va_halo", [2 * TOK, WA])
    masks_d = din("masks", [128, 4, 256], F32)
    cst_d = din("cst", [128, 16], F32)
    obT = dout("obT", [NHB, 128, TOK])
    oaT = dout("oaT", [4, 128, TOK])

    if ctx is None:
        sb = lambda name, shape, dt: nc.alloc_sbuf_tensor(name, shape, dt)
        psa = lambda name, shape, dt: nc.alloc_psum_tensor(name, shape, dt)
    else:
        sb, psa = ctx.sb, ctx.psa
    if ctx is None:
        dummy = sb("dummy_sb", [128, 16], BF16)
        P.dma("pool", "kdummy", dummy[:], cst_d, writes=["dummy"])
    ones_f = sb("ones_f", [128, 128], F32)
    ones_b = sb("ones_b", [128, 128], BF16)
    P.op("dve", lambda: nc.vector.memset(ones_f[:], 1.0), writes=["ones_f"])
    P.op("dve", lambda: nc.vector.memset(ones_b[:], 1.0), writes=["ones_b"])
    mst = sb("mst", [128, 4, 256], F32)
    masks = sb("masks_sb", [128, 4, 256], BF16)
    P.dma("sp", "k0", mst[:], masks_d, writes=["mst"])
    P.op("dve", lambda: nc.vector.tensor_copy(out=masks[:], in_=mst[:]), reads=["mst"], writes=["masks"])

    kpe = sb("kpe_sb", [64, ncores_b, TOK], BF16)
    qn = Rot("qn", [sb(f"qn{i}", [128, TOK], BF16) for i in range(2)])
    qr = Rot("qr", [sb(f"qr{i}", [64, TOK], BF16) for i in range(2)])
    kn = Rot("kn", [sb(f"kn{i}", [128, TOK], BF16) for i in range(3)])
    vv = Rot("vv", [sb(f"vv{i}", [128, TOK // 128, 128], BF16) for i in range(3)])
    pT = Rot("pT", [sb(f"pT{i}", [128, 1024], BF16) for i in range(4)])
    dacc = [sb(f"dacc{i}", [128, 1024], F32) for i in range(2)]
    rden = Rot("rden", [sb(f"rden{i}", [128, 512], F32) for i in range(2)])
    ost = Rot("ost", [sb(f"ost{i}", [128, 512], BF16) for i in range(4)])
    acc = [psa(f"acc{i}", [128, 512], F32) for i in range(4)]
    sps = Rot("sps", [psa(f"sps{i}", [128, 1024], F32) for i in range(2)])

    SC_B = 192.0 ** -0.5
    SC_A = 128.0 ** -0.5

    for r in range(ncores_b):
        P.dma("sp", "kpe_%d" % r, kpe[:, r, :], kpeT[r], writes=[("kpe", r)])
    NQT = TOK // 512
    for h in range(nheads):
        qnt, qnk = qn.next()
        qrt, qrk = qr.next()
        P.dma("sp", "qn_" + str(qnk[1]), qnt[:], QnT[h], writes=[qnk])
        P.dma("sp", "qr_" + str(qrk[1]), qrt[:], QrT[h], writes=[qrk])
        steps = []
        for r in range(ncores_b):
            knt, knk = kn.next()
            vvt, vvk = vv.next()
            P.dma("sp", "kn_" + str(knk[1]), knt[:], KnT[r, h], writes=[knk])
            P.dma("pool" if ctx is None else "sp", "vv_" + str(vvk[1]), vvt[:], Vb[r, h], writes=[vvk])
            pend = None

            def qk(c, qp):
                st, sk = sps.next()
                for hf in range(2):
                    qt = qp * 2 + hf
                    P.op("pe", lambda: nc.tensor.matmul(st[:, hf * 512:(hf + 1) * 512], lhsT=knt[:, c * 128:(c + 1) * 128],
                                                        rhs=qnt[:, qt * 512:(qt + 1) * 512], start=True, stop=False),
                         reads=[knk, qnk], writes=[sk])
                    P.op("pe", lambda: nc.tensor.matmul(st[:, hf * 512:(hf + 1) * 512], lhsT=kpe[:, r, c * 128:(c + 1) * 128],
                                                        rhs=qrt[:, qt * 512:(qt + 1) * 512], start=False, stop=True),
                         reads=[("kpe", r), qrk], writes=[sk])
                return st, sk

            def rest(c, qp, st, sk):
                pt, pk = pT.next()
                P.op("act", lambda: nc.scalar.activation(out=pt[:], in_=st[:], func=AF.Exp, scale=SC_B), reads=[sk], writes=[pk])
                first = (r == 0 and c == 0)
                last = (r == ncores_b - 1 and c == TOK // 128 - 1)
                for hf in range(2):
                    qt = qp * 2 + hf
                    P.op("pe", lambda: nc.tensor.matmul(acc[qt][:], lhsT=vvt[:, c, :], rhs=pt[:, hf * 512:(hf + 1) * 512],
                                                        start=first, stop=last),
                         reads=[vvk, pk], writes=[("acc", qt)])
                if first:
                    P.op("dve", lambda: nc.vector.tensor_copy(out=dacc[qp][:], in_=pt[:]), reads=[pk], writes=[("dacc", qp)])
                else:
                    P.op("dve", lambda: nc.vector.tensor_tensor(out=dacc[qp][:], in0=dacc[qp][:], in1=pt[:], op=ALU.add),
                         reads=[pk, ("dacc", qp)], writes=[("dacc", qp)])

            order = [(c, qp) for c in range(TOK // 128) for qp in range(NQT // 2)]
            cur = qk(*order[0])
            for i, (c, qp) in enumerate(order):
                nxt = qk(*order[i + 1]) if i + 1 < len(order) else None
                rest(c, qp, *cur)
                cur = nxt
        for qt in range(NQT):
            dt_, dk = sps.next()
            P.op("pe", lambda: nc.tensor.matmul(dt_[:, 0:512], lhsT=ones_f[:], rhs=dacc[qt // 2][:, (qt % 2) * 512:(qt % 2 + 1) * 512],
                                                start=True, stop=True),
                 reads=["ones_f", ("dacc", qt // 2)], writes=[dk])
            rt, rk = rden.next()
            P.op("dve", lambda: nc.vector.reciprocal(out=rt[:], in_=dt_[:, 0:512]), reads=[dk], writes=[rk])
            o, ok = ost.next()
            P.op("dve", lambda: nc.vector.tensor_tensor(out=o[:], in0=acc[qt][:], in1=rt[:], op=ALU.mult),
                 reads=[("acc", qt), rk], writes=[ok])
            P.dma("aq" if ctx is not None else "sp", "o_" + str(ok[1]), obT[h, :, qt * 512:(qt + 1) * 512], o[:], reads=[ok])

    if do_dil:
        qa = sb("qa_sb", [128, TOK], BF16)
        ka = sb("ka_sb", [128, 2 * TOK], BF16)
        vt = Rot("vt", [sb(f"vt{i}", [128, 17, 128], BF16) for i in range(2)])
        nd = [sb(f"nd{i}", [128, 2, TOK], F32) for i in range(3)]
        rd2 = sb("rd2", [128, TOK], F32)
        pd = Rot("pd", [sb(f"pd{i}", [128, 256], BF16) for i in range(2)])
        pm = Rot("pm", [sb(f"pm{i}", [128, 256], BF16) for i in range(2)])
        oa_st = Rot("oast", [sb(f"oast{i}", [128, TOK], BF16) for i in range(2)])
        for j in range(4):
            for g, d in enumerate(DIL):
                head = g * 4 + j
                P.dma("sp", "qa", qa[:], qaT[head], writes=["qa"])
                P.dma("sp", "ka", ka[:], kaH[head], writes=["ka"])
                L2 = 2 * TOK // d
                NT = TOK // d // 128
                n_start = 1024 // d - 64
                for rr in range(d):
                    vtt, vtk = vt.next()
                    r0 = rr + d * n_start
                    src = vaH[r0: r0 + d * (128 * (NT + 1) - 1) + 1: d, head * 128:(head + 1) * 128]
                    src = src.rearrange("(k p) v -> p k v", p=128)
                    P.dma("sp", "vt_" + str(vtk[1]), vtt[:, 0:NT + 1, :], src, writes=[vtk])
                    for i in range(NT):
                        var = (1 if i == 0 else 0) | (2 if i == NT - 1 else 0)
                        st, sk = sps.next()
                        q0 = rr + d * 128 * i
                        qcols = qa[:, q0: q0 + d * 127 + 1: d]
                        for k2 in range(2):
                            k0 = rr + d * (n_start + 128 * (i + k2))
                            P.op("pe", lambda: nc.tensor.matmul(st[:, k2 * 128:(k2 + 1) * 128], lhsT=ka[:, k0: k0 + d * 127 + 1: d], rhs=qcols,
                                                                start=True, stop=True), reads=["ka", "qa"], writes=[sk])
                        pt, pk = pd.next()
                        P.op("act", lambda: nc.scalar.activation(out=pt[:], in_=st[:, 0:256], func=AF.Exp, scale=SC_A), reads=[sk], writes=[pk])
                        pmt, pmk = pm.next()
                        P.op("dve", lambda: nc.vector.tensor_tensor(out=pmt[:], in0=pt[:], in1=masks[:, var, :], op=ALU.mult),
                             reads=[pk, "masks"], writes=[pmk])
                        at, ak = sps.next()
                        for k2 in range(2):
                            P.op("pe", lambda: nc.tensor.matmul(at[:, 0:128], lhsT=vtt[:, i + k2, :], rhs=pmt[:, k2 * 128:(k2 + 1) * 128],
                                                                start=(k2 == 0), stop=(k2 == 1)), reads=[vtk, pmk], writes=[ak])
                        for k2 in range(2):
                            P.op("pe", lambda: nc.tensor.matmul(at[:, 128:256], lhsT=ones_b[:], rhs=pmt[:, k2 * 128:(k2 + 1) * 128],
                                                                start=(k2 == 0), stop=(k2 == 1)), reads=["ones_b", pmk], writes=[ak])
                        P.op("act", lambda: nc.scalar.copy(out=nd[g][:, :, q0: q0 + d * 127 + 1: d],
                                                           in_=at[:, 0:256].rearrange("p (a b) -> p a b", b=128)),
                             reads=[ak], writes=[("nd", g)])
            P.op("dve", lambda: nc.vector.tensor_tensor(out=nd[0][:], in0=nd[0][:], in1=nd[1][:], op=ALU.add),
                 reads=[("nd", 0), ("nd", 1)], writes=[("nd", 0)])
            P.op("dve", lambda: nc.vector.tensor_tensor(out=nd[0][:], in0=nd[0][:], in1=nd[2][:], op=ALU.add),
                 reads=[("nd", 0), ("nd", 2)], writes=[("nd", 0)])
            P.op("dve", lambda: nc.vector.reciprocal(out=rd2[:], in_=nd[0][:, 1, :]), reads=[("nd", 0)], writes=["rd2"])
            o, ok = oa_st.next()
            P.op("dve", lambda: nc.vector.tensor_tensor(out=o[:], in0=nd[0][:, 0, :], in1=rd2[:], op=ALU.mult),
                 reads=[("nd", 0), "rd2"], writes=[ok])
            P.dma("aq" if ctx is not None else "sp", "oa_" + str(ok[1]), oaT[j], o[:], reads=[ok])

    if ctx is not None:
        return (sps.bufs[0][0:1, 0:1], ("sps", 0))
    P.finish(["o_", "oa_"])
    return nc


class NormHelper:
    def __init__(self, nc, P, TT, ones, cst, psw, sqb, rtmp, rstd, tag=""):
        self.nc, self.P, self.TT = nc, P, TT
        self.ones, self.cst, self.psw, self.sqb, self.rtmp, self.rstd = ones, cst, psw, sqb, rtmp, rstd
        self.tag = tag
        self.pw = None

    def begin(self):
        self.pw = [self.psw.next() for _ in range(self.TT // 512)]

    def add(self, ap, key, c, nchunks):
        nc, P = self.nc, self.P
        sq, sqk = self.sqb.next()
        P.op("act", lambda: nc.scalar.activation(out=sq[:], in_=ap, func=AF.Square), reads=[key], writes=[sqk])
        for hf in range(self.TT // 512):
            pt, pk = self.pw[hf]
            P.op("pe", lambda: nc.tensor.matmul(pt[:], lhsT=self.ones[:], rhs=sq[:, hf * 512:(hf + 1) * 512],
                                                start=(c == 0), stop=(c == nchunks - 1)),
                 reads=["ones", sqk], writes=[pk])

    def finish(self, nchunks):
        nc, P = self.nc, self.P
        for hf in range(self.TT // 512):
            pt, pk = self.pw[hf]
            sl = slice(hf * 512, (hf + 1) * 512)
            P.op("act", lambda: nc.scalar.activation(out=self.rtmp[:, sl], in_=pt[:], func=AF.Sqrt,
                                                     scale=1.0 / (128 * nchunks), bias=self.cst[:, 5:6]),
                 reads=[pk, "cst"], writes=[("rtmp" + self.tag, hf)])
            P.op("dve", lambda: nc.vector.reciprocal(out=self.rstd[:, sl], in_=self.rtmp[:, sl]),
                 reads=[("rtmp" + self.tag, hf)], writes=[("rstd" + self.tag, hf)])
        return [("rstd" + self.tag, hf) for hf in range(self.TT // 512)]


def build_phase_c1(ctx=None):
    if ctx is None:
        nc = bass.Bass("TRN2", target_bir_lowering=False)
        P = Prog(nc)
        sfx = ""
    else:
        nc, P, sfx = ctx.nc, ctx.P, ctx.sfx
    TT = 512
    NTT = TOK // TT

    def din(name, shape, dt=F32):
        if ctx is not None:
            ap = ctx.t[name]
            assert list(ap.shape) == list(shape), (name, ap.shape, shape)
            return ap
        return nc.dram_tensor(name, shape, dt, kind="ExternalInput").ap()

    def dout(name, shape, dt=BF16):
        if ctx is not None:
            ap = ctx.t[name]
            assert list(ap.shape) == list(shape), (name, ap.shape, shape)
            return ap
        return nc.dram_tensor(name, shape, dt, kind="ExternalOutput").ap()

    xT = din("xT", [DC, 128, TOK])
    gmix = din("gmix", [128, DC])
    gffn = din("gffn", [128, DC])
    bg = din("bgate", [128, 2, DC])
    w_in = din("w_in", [D, NIN])
    w_oa = din("w_oa", [512, D])
    w_ob = din("w_ob", [D, D])
    w_out = din("w_out", [D, D])
    oaT = din("oaT", [4, 128, TOK], BF16)
    obT = din("obT", [NHB, 128, TOK], BF16)
    cst_d = din("cst", [128, 16])
    xmT = dout("xmT", [DC, 128, TOK], F32)
    h2T = dout("h2T", [DC, 128, TOK], BF16)

    if ctx is None:
        sb = lambda name, shape, dt: nc.alloc_sbuf_tensor(name, shape, dt)
        psa = lambda name, shape, dt: nc.alloc_psum_tensor(name, shape, dt)
    else:
        sb, psa = ctx.sb, ctx.psa
    if ctx is None:
        dummy = sb("dummy_sb", [128, 16], BF16)
        P.dma("pool", "kdummy", dummy[:], cst_d, writes=["dummy"])
    cst = sb("cst_sb", [128, 16], F32)
    gm = sb("gm_sb", [128, DC], F32)
    gf = sb("gf_sb", [128, DC], F32)
    bgs = sb("bg_sb", [128, 2, DC], F32)
    ones = sb("ones_sb", [128, 128], F32)
    P.dma("sp", "k0", cst[:], cst_d, writes=["cst"])
    P.dma("sp", "k1", gm[:], gmix, writes=["gm"])
    P.dma("sp", "k2", gf[:], gffn, writes=["gf"])
    P.dma("sp", "k3", bgs[:], bg, writes=["bg"])
    P.op("dve", lambda: nc.vector.memset(ones[:], 1.0), writes=["ones"])

    hT = sb("hT", [128, DC, TT], BF16)
    obs = sb("obs", [128, NHB, TT], BF16)
    oas = sb("oas", [128, 4, TT], BF16)
    mg = sb("mg", [128, DC, TT], BF16)
    xm = sb("xm", [128, DC, TT], F32)
    sA = sb("sA", [128, 4, TT], F32)
    mA = sb("mA", [128, 4, TT], F32)
    tB = Rot("tB", [sb(f"tB{i}", [128, TT], F32) for i in range(2)])
    wbufs = Rot("wbuf", [sb(f"wbuf{i}", [128, DC, 512], BF16) for i in range(3)])
    xst = Rot("xst", [sb(f"xst{i}", [128, TT], F32) for i in range(3)])
    xsd = Rot("xsd", [sb(f"xsd{i}", [128, TT], F32) for i in range(3)])
    sqb = Rot("sqb", [sb(f"sqb{i}", [128, TT], F32) for i in range(2)])
    rstd = sb("rstd", [128, TT], F32)
    rtmp = sb("rtmp", [128, TT], F32)
    ost = Rot("ost", [sb(f"ost{i}", [128, TT], BF16) for i in range(3)])
    psum = Rot("ps", [psa(f"ps{i}", [128, 512], F32) for i in range(6)])
    psw = Rot("psw", [psa(f"psw{i}", [128, 512], F32) for i in range(2)])
    NH = NormHelper(nc, P, TT, ones, cst, psw, sqb, rtmp, rstd)

    w_in_v = w_in.rearrange("(kc k) n -> k kc n", k=128)
    w_oa_v = w_oa.rearrange("(kc k) n -> k kc n", k=128)
    w_ob_v = w_ob.rearrange("(kc k) n -> k kc n", k=128)
    w_out_v = w_out.rearrange("(kc k) n -> k kc n", k=128)

    def load_w(dst_ap, src_ap, key):
        return P.dma("pool" if ctx is None else "sp", "w_" + str(key[1]), dst_ap, src_ap, writes=[key])

    def mm_group(wt, wk, col0, rhs, rkeys, kch):
        pt, pk = psum.next()
        for kc in range(kch):
            P.op("pe", lambda: nc.tensor.matmul(pt[:], lhsT=wt[:, kc, col0:col0 + 128], rhs=rhs[:, kc, :],
                                                start=(kc == 0), stop=(kc == kch - 1)), reads=[wk] + rkeys, writes=[pk])
        return pt, pk

    for tt in range(NTT):
        tsl = slice(tt * TT, (tt + 1) * TT)
        P.dma("sp", "ob", obs[:], obT[:, :, tsl].rearrange("h p t -> p h t"), writes=["obs"])
        P.dma("sp", "oa", oas[:], oaT[:, :, tsl].rearrange("h p t -> p h t"), writes=["oas"])
        NH.begin()
        for c in range(DC):
            t, k = xst.next()
            P.dma("sp", "x_" + str(k[1]), t[:], xT[c, :, tsl], writes=[k])
            NH.add(t[:], k, c, DC)
        rk = NH.finish(DC)
        for c in range(DC):
            t, k = xsd.next()
            P.dma("sp", "xd_" + str(k[1]), t[:], xT[c, :, tsl], writes=[k])
            P.op("dve", lambda: nc.vector.scalar_tensor_tensor(out=hT[:, c, :], in0=t[:], scalar=gm[:, c:c + 1], in1=rstd[:],
                                                               op0=ALU.mult, op1=ALU.mult),
                 reads=[k, "gm"] + rk, writes=[("hT", c)])
        hkeys = [("hT", c) for c in range(DC)]
        for fbq in range(4):
            wt, wk = wbufs.next()
            load_w(wt[:], w_in_v[:, :, O_G + fbq * 512: O_G + (fbq + 1) * 512], wk)
            for sub in range(4):
                fb = fbq * 4 + sub
                pt, pk = mm_group(wt, wk, sub * 128, hT, hkeys, DC)
                P.op("act", lambda: nc.scalar.activation(out=sA[:, sub, :], in_=pt[:], func=AF.Sigmoid, bias=bgs[:, 0, fb:fb + 1]),
                     reads=[pk, "bg"], writes=[("sA", sub)])
            wt, wk = wbufs.next()
            load_w(wt[:, 0:4, :], w_oa_v[:, :, fbq * 512:(fbq + 1) * 512], wk)
            for sub in range(4):
                pt, pk = mm_group(wt, wk, sub * 128, oas, ["oas"], 4)
                P.op("dve", lambda: nc.vector.tensor_tensor(out=mA[:, sub, :], in0=pt[:], in1=sA[:, sub, :], op=ALU.mult),
                     reads=[pk, ("sA", sub)], writes=[("mA", sub)])
            wt, wk = wbufs.next()
            load_w(wt[:], w_in_v[:, :, O_G + D + fbq * 512: O_G + D + (fbq + 1) * 512], wk)
            for sub in range(4):
                fb = fbq * 4 + sub
                pt, pk = mm_group(wt, wk, sub * 128, hT, hkeys, DC)
                P.op("act", lambda: nc.scalar.activation(out=sA[:, sub, :], in_=pt[:], func=AF.Sigmoid, bias=bgs[:, 1, fb:fb + 1]),
                     reads=[pk, "bg"], writes=[("sA", sub)])
            wt, wk = wbufs.next()
            load_w(wt[:], w_ob_v[:, :, fbq * 512:(fbq + 1) * 512], wk)
            for sub in range(4):
                fb = fbq * 4 + sub
                pt, pk = mm_group(wt, wk, sub * 128, obs, ["obs"], DC)
                tb, tbk = tB.next()
                P.op("dve", lambda: nc.vector.tensor_tensor(out=tb[:], in0=pt[:], in1=sA[:, sub, :], op=ALU.mult),
                     reads=[pk, ("sA", sub)], writes=[tbk])
                P.op("dve", lambda: nc.vector.tensor_tensor(out=mg[:, fb, :], in0=tb[:], in1=mA[:, sub, :], op=ALU.add),
                     reads=[tbk, ("mA", sub)], writes=[("mg", fb)])
        mkeys = [("mg", c) for c in range(DC)]
        NH.begin()
        for fbq in range(4):
            wt, wk = wbufs.next()
            load_w(wt[:], w_out_v[:, :, fbq * 512:(fbq + 1) * 512], wk)
            for sub in range(4):
                fb = fbq * 4 + sub
                pt, pk = mm_group(wt, wk, sub * 128, mg, mkeys, DC)
                t, k = xsd.next()
                P.dma("sp", "xd_" + str(k[1]), t[:], xT[fb, :, tsl], writes=[k])
                P.op("dve", lambda: nc.vector.tensor_tensor(out=xm[:, fb, :], in0=pt[:], in1=t[:], op=ALU.add),
                     reads=[pk, k], writes=[("xm", fb)])
                P.dma("aq" if ctx is not None else "sp", "xm_%d" % fb, xmT[fb, :, tsl], xm[:, fb, :], reads=[("xm", fb)])
                NH.add(xm[:, fb, :], ("xm", fb), fb, DC)
        rk = NH.finish(DC)
        for c in range(DC):
            o, ok = ost.next()
            P.op("dve", lambda: nc.vector.scalar_tensor_tensor(out=o[:], in0=xm[:, c, :], scalar=gf[:, c:c + 1], in1=rstd[:],
                                                               op0=ALU.mult, op1=ALU.mult),
                 reads=[("xm", c), "gf"] + rk, writes=[ok])
            P.dma("aq" if ctx is not None else "sp", "o_" + str(ok[1]), h2T[c, :, tsl], o[:], reads=[ok])

    if ctx is not None:
        return (psw.bufs[0][0:1, 0:1], ("psw", 0))
    P.finish(["o_", "xm_"])
    return nc


def build_phase_c2(final, ctx=None):
    if ctx is None:
        nc = bass.Bass("TRN2", target_bir_lowering=False)
        P = Prog(nc)
        sfx = ""
    else:
        nc, P, sfx = ctx.nc, ctx.P, ctx.sfx
    TT = 512
    NTT = TOK // TT

    def din(name, shape, dt=F32):
        if ctx is not None:
            ap = ctx.t[name]
            assert list(ap.shape) == list(shape), (name, ap.shape, shape)
            return ap
        return nc.dram_tensor(name, shape, dt, kind="ExternalInput").ap()

    def dout(name, shape, dt=BF16):
        if ctx is not None:
            ap = ctx.t[name]
            assert list(ap.shape) == list(shape), (name, ap.shape, shape)
            return ap
        return nc.dram_tensor(name, shape, dt, kind="ExternalOutput").ap()

    h2x_d = din("h2x", [DC, 128, TOK + 2], BF16)
    xmT = din("xmT", [DC, 128, TOK])
    w_up = din("w_up", [D, 2 * DFF])
    w_down = din("w_down", [DFF, D])
    cw = din("conv_w", [128, 3, 2 * FC])
    cb = din("conv_b", [128, 2 * FC])
    gfin = din("gfin", [128, DC])
    cst_d = din("cst", [128, 16])
    xoT = dout("xoT", [DC, 128, TOK], F32)

    if ctx is None:
        sb = lambda name, shape, dt: nc.alloc_sbuf_tensor(name, shape, dt)
        psa = lambda name, shape, dt: nc.alloc_psum_tensor(name, shape, dt)
    else:
        sb, psa = ctx.sb, ctx.psa
    if ctx is None:
        dummy = sb("dummy_sb", [128, 16], BF16)
        P.dma("pool", "kdummy", dummy[:], cst_d, writes=["dummy"])
    cst = sb("cst_sb", [128, 16], F32)
    cws = sb("cw_sb", [128, 3, 2 * FC], F32)
    cbs = sb("cb_sb", [128, 2 * FC], F32)
    gfs = sb("gfin_sb", [128, DC], F32)
    ones = sb("ones_sb", [128, 128], F32)
    P.dma("sp", "k0", cst[:], cst_d, writes=["cst"])
    P.dma("sp", "k1", cws[:], cw, writes=["cw"])
    P.dma("sp", "k2", cbs[:], cb, writes=["cb"])
    P.dma("sp", "k3", gfs[:], gfin, writes=["gfin"])
    P.op("dve", lambda: nc.vector.memset(ones[:], 1.0), writes=["ones"])

    h2x = sb("h2x_sb", [128, DC, TT + 2], BF16)
    gT = sb("gT", [128, FC, TT], BF16)
    wbufs = Rot("wbuf", [sb(f"wbuf{i}", [128, DC, 512], BF16) for i in range(3)])
    uext = Rot("uext", [sb(f"uext{i}", [128, TT + 2], F32) for i in range(4)])
    ua = Rot("ua", [sb(f"ua{i}", [128, TT], F32) for i in range(2)])
    ub = Rot("ub", [sb(f"ub{i}", [128, TT], F32) for i in range(2)])
    sa = Rot("sa", [sb(f"sa{i}", [128, TT], F32) for i in range(2)])
    xst = Rot("xst", [sb(f"xst{i}", [128, TT], F32) for i in range(2)])
    xo = sb("xo", [128, DC, TT], F32)
    sqb = Rot("sqb", [sb(f"sqb{i}", [128, TT], F32) for i in range(2)])
    rstd = sb("rstd", [128, TT], F32)
    rtmp = sb("rtmp", [128, TT], F32)
    ost = Rot("ost", [sb(f"ost{i}", [128, TT], F32) for i in range(2)])
    pmain = Rot("pm", [psa(f"pm{i}", [128, 512], F32) for i in range(2)])
    phalo = Rot("ph", [psa(f"ph{i}", [128, 512], F32) for i in range(2)])
    pdown = [psa(f"pd{i}", [128, 512], F32) for i in range(4)]
    NH = NormHelper(nc, P, TT, ones, cst, pmain, sqb, rtmp, rstd)

    w_up_v = w_up.rearrange("(kc k) n -> k kc n", k=128)
    w_dn_v = w_down.rearrange("(kc k) n -> k kc n", k=128)

    def load_w(dst_ap, src_ap, key):
        return P.dma("pool" if ctx is None else "sp", "w_" + str(key[1]), dst_ap, src_ap, writes=[key])

    hkeys = ["h2x"]
    halo_slot = [0]

    def up_conv(wt, wk, sub, chunk, dst_rot):
        pt, pk = pmain.next()
        for kc in range(DC):
            P.op("pe", lambda: nc.tensor.matmul(pt[:], lhsT=wt[:, kc, sub * 128:(sub + 1) * 128], rhs=h2x[:, kc, 1:TT + 1],
                                                start=(kc == 0), stop=(kc == DC - 1)), reads=[wk] + hkeys, writes=[pk])
        ph, phk = phalo.next()
        hs = 0
        phv = ph[:, 0:2]
        for kc in range(DC):
            P.op("pe", lambda: nc.tensor.matmul(phv, lhsT=wt[:, kc, sub * 128:(sub + 1) * 128], rhs=h2x[:, kc, 0:TT + 2:TT + 1],
                                                start=(kc == 0), stop=(kc == DC - 1)), reads=[wk] + hkeys, writes=[phk])
        ue, uek = uext.next()
        P.op("act", lambda: nc.scalar.copy(out=ue[:, 1:TT + 1], in_=pt[:]), reads=[pk], writes=[uek])
        P.op("act", lambda: nc.scalar.copy(out=ue[:, 0:TT + 2:TT + 1], in_=phv), reads=[phk, uek], writes=[uek])
        u, uk = dst_rot.next()
        P.op("dve", lambda: nc.vector.tensor_scalar(out=u[:], in0=ue[:, 1:TT + 1], scalar1=cws[:, 1, chunk:chunk + 1],
                                                    scalar2=cbs[:, chunk:chunk + 1], op0=ALU.mult, op1=ALU.add),
             reads=[uek, "cw", "cb"], writes=[uk])
        P.op("dve", lambda: nc.vector.scalar_tensor_tensor(out=u[:], in0=ue[:, 0:TT], scalar=cws[:, 0, chunk:chunk + 1], in1=u[:],
                                                           op0=ALU.mult, op1=ALU.add), reads=[uek, "cw", uk], writes=[uk])
        P.op("dve", lambda: nc.vector.scalar_tensor_tensor(out=u[:], in0=ue[:, 2:TT + 2], scalar=cws[:, 2, chunk:chunk + 1], in1=u[:],
                                                           op0=ALU.mult, op1=ALU.add), reads=[uek, "cw", uk], writes=[uk])
        return u, uk

    for tt in range(NTT):
        tsl = slice(tt * TT, (tt + 1) * TT)
        P.dma("sp", "h2x", h2x[:], h2x_d[:, :, tt * TT: tt * TT + TT + 2].rearrange("c p t -> p c t"), writes=["h2x"])
        for ib in range(DFF // 512):
            wa, wak = wbufs.next()
            load_w(wa[:], w_up_v[:, :, ib * 512:(ib + 1) * 512], wak)
            wb_, wbk = wbufs.next()
            load_w(wb_[:], w_up_v[:, :, DFF + ib * 512: DFF + (ib + 1) * 512], wbk)
            for sub in range(4):
                j = ib * 4 + sub
                u_a, uak = up_conv(wa, wak, sub, j, ua)
                u_b, ubk = up_conv(wb_, wbk, sub, FC + j, ub)
                s_, sk = sa.next()
                P.op("act", lambda: nc.scalar.activation(out=s_[:], in_=u_a[:], func=AF.Silu), reads=[uak], writes=[sk])
                P.op("dve", lambda: nc.vector.tensor_tensor(out=gT[:, j, :], in0=s_[:], in1=u_b[:], op=ALU.mult),
                     reads=[sk, ubk], writes=[("gT", j)])
        gkeys = [("gT", j) for j in range(FC)]
        if final:
            NH.begin()
        for fbq in range(4):
            for kg in range(4):
                wt, wk = wbufs.next()
                load_w(wt[:, 0:11, :], w_dn_v[:, kg * 11:(kg + 1) * 11, fbq * 512:(fbq + 1) * 512], wk)
                for sub in range(4):
                    for kc in range(11):
                        P.op("pe", lambda: nc.tensor.matmul(pdown[sub][:], lhsT=wt[:, kc, sub * 128:(sub + 1) * 128], rhs=gT[:, kg * 11 + kc, :],
                                                            start=(kg == 0 and kc == 0), stop=(kg == 3 and kc == 10)),
                             reads=[wk] + gkeys, writes=[("pdown", sub)])
            for sub in range(4):
                fb = fbq * 4 + sub
                t, k = xst.next()
                P.dma("sp", "x_" + str(k[1]), t[:], xmT[fb, :, tsl], writes=[k])
                if final:
                    P.op("dve", lambda: nc.vector.tensor_tensor(out=xo[:, fb, :], in0=pdown[sub][:], in1=t[:], op=ALU.add),
                         reads=[("pdown", sub), k], writes=[("xo", fb)])
                    NH.add(xo[:, fb, :], ("xo", fb), fb, DC)
                else:
                    o, ok = ost.next()
                    P.op("dve", lambda: nc.vector.tensor_tensor(out=o[:], in0=pdown[sub][:], in1=t[:], op=ALU.add),
                         reads=[("pdown", sub), k], writes=[ok])
                    P.dma("aq" if ctx is not None else "sp", "o_" + str(ok[1]), xoT[fb, :, tsl], o[:], reads=[ok])
        if final:
            rk = NH.finish(DC)
            for c in range(DC):
                o, ok = ost.next()
                P.op("dve", lambda: nc.vector.scalar_tensor_tensor(out=o[:], in0=xo[:, c, :], scalar=gfs[:, c:c + 1], in1=rstd[:],
                                                                   op0=ALU.mult, op1=ALU.mult),
                     reads=[("xo", c), "gfin"] + rk, writes=[ok])
                P.dma("aq" if ctx is not None else "sp", "o_" + str(ok[1]), xoT[c, :, tsl], o[:], reads=[ok])

    if ctx is not None:
        return (pmain.bufs[0][0:1, 0:1], ("pm", 0))
    P.finish(["o_"])
    return nc


class Ctx:
    ARENA = 198 * 1024

    def __init__(self, nc, P):
        self.nc, self.P = nc, P
        self.t = {}
        self.sfx = ""
        self.arena = nc.alloc_sbuf_tensor("arena", [128, self.ARENA // 2], BF16)
        self.banks = [nc.alloc_psum_tensor("bank%d" % i, [128, 1024], F32) for i in range(4)]
        self.off = 0
        self.nb = 0

    def begin(self):
        self.off = 0
        self.nb = 0

    def sb(self, name, shape, dt):
        esz = 4 if dt in (F32, I32) else 2
        n = 1
        for d in shape[1:]:
            n *= d
        nbytes = (n * esz + 31) // 32 * 32
        assert self.off + nbytes <= self.ARENA, ("arena overflow", name, self.off, nbytes)
        v = self.arena[0:shape[0], self.off // 2:(self.off + n * esz) // 2]
        if dt != BF16:
            v = v.bitcast(dt)
        self.off += nbytes
        if len(shape) == 3:
            v = v.rearrange("p (a b) -> p a b", b=shape[2])
        elif len(shape) == 4:
            v = v.rearrange("p (a b c) -> p a b c", b=shape[2], c=shape[3])
        return v

    def psa(self, name, shape, dt):
        assert dt == F32 and list(shape) in ([128, 512], [128, 1024])
        if shape[1] == 1024:
            self.nb = (self.nb + 1) // 2 * 2
            b = self.banks[self.nb // 2][:]
            self.nb += 2
            return b
        b = self.banks[self.nb // 2][:, (self.nb % 2) * 512:(self.nb % 2 + 1) * 512]
        self.nb += 1
        return b


def build_fused(nvc=NCORES, depth=DEPTH):
    nc = bass.Bass("TRN2", target_bir_lowering=False)
    P = Prog(nc)
    ctx = Ctx(nc, P)
    E4 = [DC, 128, TOK]

    def ext(name, shape, dt=F32):
        return nc.dram_tensor(name, shape, dt, kind="ExternalInput").ap()

    def scr(name, shape, dt=BF16):
        return nc.dram_tensor(name, shape, dt).ap()

    xT0 = ext("xT", [nvc] + E4)
    pos = ext("pos", [nvc, 64, TOK], I32)
    masks = ext("masks", [nvc, 128, 4, 256])
    cst_d = ext("cst", [128, 16])
    perm128_d = ext("perm128", [128, 128])
    perm64_d = ext("perm64", [128, 128])
    gmix = ext("gmix", [depth, 128, DC])
    gffn = ext("gffn", [depth, 128, DC])
    gfin = ext("gfin", [128, DC])
    bgate = ext("bgate", [depth, 128, 2, DC])
    nq = ext("nq", [depth, 128, 4])
    nkv = ext("nkv", [depth, 128, 4])
    cw = ext("conv_w", [depth, 128, 3, 2 * FC])
    cb = ext("conv_b", [depth, 128, 2 * FC])
    w_in = ext("w_in", [depth, D, NIN])
    w_uq = ext("w_uq", [depth, QL, 3072])
    w_ukv = ext("w_ukv", [depth, KVL, 4096])
    w_oa = ext("w_oa", [depth, 512, D])
    w_ob = ext("w_ob", [depth, D, D])
    w_out = ext("w_out", [depth, D, D])
    w_up = ext("w_up", [depth, D, 2 * DFF])
    w_down = ext("w_down", [depth, DFF, D])
    OUT = nc.dram_tensor("out", [nvc] + E4, F32, kind="ExternalOutput").ap()

    X1 = scr("X1", [nvc] + E4, F32)
    XM = scr("XM", [nvc] + E4, F32)
    QA = scr("QA", [nvc] + E4)
    KAB = scr("KAB", [nvc + 2] + E4)
    VAB = scr("VAB", [nvc + 2, TOK, 2048])
    QN = scr("QN", [nvc] + E4)
    QR = scr("QR", [nvc] + E4)
    KN = scr("KN", [nvc] + E4)
    KPEB = scr("KPEB", [nvc] + E4)
    KPE = KPEB[:, 0, 0:64, :]
    VB = scr("VB", [nvc, NHB, 128, TOK // 128, 128])
    OB = scr("OB", [nvc] + E4)
    OA = scr("OA", [nvc] + E4)
    H2B = scr("H2B", [nvc + 2] + E4)
    s_x = scr("s_x", E4, F32)
    s_pos = scr("s_pos", [128, TOK], I32)
    s_msk = scr("s_msk", [128, 4, 256], F32)
    s_qa = scr("s_qa", [12, 128, TOK])
    s_ka = scr("s_ka", [12, 128, TOK])
    s_va = scr("s_va", [TOK, WA])
    s_qn = scr("s_qn", [NHB, 128, TOK])
    s_qr = scr("s_qr", [NHB // 2, 128, TOK])
    s_kn = scr("s_kn", [NHB, 128, TOK])
    s_kpe = scr("s_kpe", [64, TOK])
    s_vb = scr("s_vb", [NHB, 128, TOK // 128, 128])
    s_kaw = scr("s_kaw", [12, 128, 2 * TOK])
    s_vaw = scr("s_vaw", [2 * TOK, WA])
    s_ob = scr("s_ob", [NHB, 128, TOK])
    s_oa = scr("s_oa", [4, 128, TOK])
    s_xm = scr("s_xm", E4, F32)
    s_h2 = scr("s_h2", E4)
    s_h2x = scr("s_h2x", [DC, 128, TOK + 2])
    s_xo = scr("s_xo", E4, F32)
    s_kb = scr("s_kb", [3] + E4)
    s_vb3 = scr("s_vb3", [3, TOK, 2048])
    s_hb = scr("s_hb", [3] + E4)

    def cp(q, out, in_, tag, **kw):
        P.dma(q, "cp_" + tag, out, in_, **kw)

    ctx.begin()
    dummy = ctx.sb("dummy_sb", [128, 16], BF16)
    P.dma("pool", "kdummy", dummy[:], cst_d, writes=["dummy"])
    z = ctx.sb("zeros_sb", [128, 2048], BF16)
    P.op("dve", lambda: nc.vector.memset(z[:], 0.0), writes=["z"])
    fps = ctx.psa("fence_ps", [128, 512], F32)
    for blk in (0, nvc + 1):
        for c in range(DC):
            P.dma("sp", "init0", KAB[blk, c], z[:, :], reads=["z"])
            P.dma("sp", "init1", H2B[blk, c], z[:, :], reads=["z"])
            P.dma("sp", "init2", VAB[blk, c * 128:(c + 1) * 128, :], z[:, :], reads=["z"])
    wb = {}
    for nm, t in (("w_in", w_in), ("w_uq", w_uq), ("w_ukv", w_ukv), ("w_oa", w_oa), ("w_ob", w_ob),
                  ("w_out", w_out), ("w_up", w_up), ("w_down", w_down)):
        tb = scr(nm + "_bf", list(t.shape), BF16)
        wb[nm] = tb
        rows = t.shape[1]
        for l_ in range(depth):
            for r0 in range(0, rows, 128):
                P.dma("pool", "cv_%s" % nm, tb[l_, r0:r0 + 128, :], t[l_, r0:r0 + 128, :])
    w_in, w_uq, w_ukv, w_oa, w_ob, w_out, w_up, w_down = (wb[k] for k in ("w_in", "w_uq", "w_ukv", "w_oa", "w_ob", "w_out", "w_up", "w_down"))
    P.hard_reset((fps[0:1, 0:1], "fence_ps", P.relay_w[:, 0:1]))

    inst = [0]
    last_ps = [None]

    def run(fn, tensors, pre, post, **kw):
        inst[0] += 1
        ctx.sfx = "_i%d" % inst[0]
        ctx.t = tensors
        ctx.begin()
        fz = ctx.psa("fence_ps", [128, 512], F32)
        pre()
        P.fence((fz[0:1, 0:1], "fence_ps", P.relay_w[:, 0:1]))
        ctx.begin()
        ps, pskey = fn(ctx=ctx, **kw)
        P.fence((ps, pskey, P.relay_w[:, 0:1]))
        post()
        P.fence((ps, pskey, P.relay_w[:, 0:1]))
        last_ps[0] = (ps, pskey, P.relay_w[:, 0:1])

    for l in range(depth):
        Xin = xT0 if l == 0 else X1
        last = (l == depth - 1)
        with nc.Fori(0, nvc) as vc:
            def pre_a():
                cp("sp", s_x, Xin[vc], "x")
                cp("aq", s_pos[0:64], pos[vc], "pos0")
                cp("aq", s_pos[64:128], pos[vc], "pos1")

            def post_a():
                cp("sp", QA[vc][0:12], s_qa, "qa")
                cp("aq", KAB[1:][vc][0:12], s_ka, "ka")
                cp("aq", VAB[1:][vc][:, 0:WA], s_va, "va")
                cp("sp", QN[vc], s_qn, "qn")
                cp("sp", QR[vc][0:NHB // 2], s_qr, "qr")
                cp("sp", KN[vc], s_kn, "kn")
                cp("sp", KPEB[vc][0, 0:64, :], s_kpe, "kpe")
                cp("sp", VB[vc], s_vb, "vb")
            run(build_phase_a, {
                "xT": s_x, "pos": s_pos, "gmix": gmix[l], "w_in": w_in[l], "nq": nq[l], "nkv": nkv[l],
                "w_uq": w_uq[l], "w_ukv": w_ukv[l], "cst": cst_d, "perm128": perm128_d, "perm64": perm64_d,
                "qaT": s_qa, "kaT": s_ka, "va": s_va, "QnT": s_qn, "QrT": s_qr, "KnT": s_kn, "kpeT": s_kpe, "Vb": s_vb},
                pre_a, post_a)
            P.hard_reset(last_ps[0])
        with nc.Fori(0, nvc) as vc:
            def pre_b():
                cp("sp", s_qn, QN[vc], "qn")
                cp("sp", s_qr, QR[vc][0:NHB // 2], "qr")
                cp("sp", s_qa, QA[vc][0:12], "qa")
                cp("sp", s_kb[0], KAB[vc], "kb0", writes=["kb0"])
                cp("aq", s_kb[1], KAB[1:][vc], "kb1", writes=["kb1"])
                cp("aq", s_kb[2], KAB[2:][vc], "kb2", writes=["kb2"])
                cp("sp", s_vb3[0], VAB[vc], "vb0", writes=["vb0"])
                cp("aq", s_vb3[1], VAB[1:][vc], "vb1", writes=["vb1"])
                cp("aq", s_vb3[2], VAB[2:][vc], "vb2", writes=["vb2"])
                cp("sp", s_kaw[:, :, 0:1024], s_kb[0, 0:12, :, 1024:2048], "k0", reads=["kb0"])
                cp("sp", s_kaw[:, :, 1024:3072], s_kb[1, 0:12], "k1", reads=["kb1"])
                cp("sp", s_kaw[:, :, 3072:4096], s_kb[2, 0:12, :, 0:1024], "k2", reads=["kb2"])
                cp("sp", s_vaw[0:1024, :], s_vb3[0, 1024:2048, 0:WA], "v0", reads=["vb0"])
                cp("sp", s_vaw[1024:3072, :], s_vb3[1, :, 0:WA], "v1", reads=["vb1"])
                cp("sp", s_vaw[3072:4096, :], s_vb3[2, 0:1024, 0:WA], "v2", reads=["vb2"])
                cp("aq", s_msk, masks[vc], "msk")

            def post_b():
                cp("sp", OB[vc], s_ob, "ob")
                cp("sp", OA[vc][0:4], s_oa, "oa")
            run(build_phase_b, {
                "QnT": s_qn, "QrT": s_qr.rearrange("a (b p) t -> (a b) p t", b=2), "KnT_all": KN, "kpeT_all": KPE, "Vb_all": VB,
                "qaT": s_qa, "kaT_halo": s_kaw, "va_halo": s_vaw, "masks": s_msk, "cst": cst_d, "obT": s_ob, "oaT": s_oa},
                pre_b, post_b, ncores_b=nvc)

            def pre_c1():
                cp("sp", s_x, Xin[vc], "x")

            def post_c1():
                cp("sp", XM[vc], s_xm, "xm")
                cp("aq", H2B[1:][vc], s_h2, "h2")
            run(build_phase_c1, {
                "xT": s_x, "gmix": gmix[l], "gffn": gffn[l], "bgate": bgate[l], "w_in": w_in[l], "w_oa": w_oa[l],
                "w_ob": w_ob[l], "w_out": w_out[l], "oaT": s_oa, "obT": s_ob, "cst": cst_d, "xmT": s_xm, "h2T": s_h2},
                pre_c1, post_c1)
            P.hard_reset(last_ps[0])
        with nc.Fori(0, nvc) as vc:
            def pre_c2():
                cp("sp", s_xm, XM[vc], "xm")
                cp("sp", s_hb[0], H2B[vc], "hb0", writes=["hb0"])
                cp("aq", s_hb[1], H2B[1:][vc], "hb1", writes=["hb1"])
                cp("aq", s_hb[2], H2B[2:][vc], "hb2", writes=["hb2"])
                cp("sp", s_h2x[:, :, 1:TOK + 1], s_hb[1], "h2a", reads=["hb1"])
                cp("sp", s_h2x[:, :, 0:1], s_hb[0, :, :, TOK - 1:TOK], "h2b", reads=["hb0"], allow_slow_non_contiguous=True)
                cp("sp", s_h2x[:, :, TOK + 1:TOK + 2], s_hb[2, :, :, 0:1], "h2c", reads=["hb2"], allow_slow_non_contiguous=True)

            def post_c2():
                cp("sp", (OUT[vc] if last else X1[vc]), s_xo, "xo")
            run(build_phase_c2, {
                "h2x": s_h2x, "xmT": s_xm, "w_up": w_up[l], "w_down": w_down[l],
                "conv_w": cw[l], "conv_b": cb[l], "gfin": gfin, "cst": cst_d, "xoT": s_xo},
                pre_c2, post_c2, final=last)
            P.hard_reset(last_ps[0])
    return nc


def _fm2(v, nchunk):
    v = np.asarray(v, np.float32)
    return np.ascontiguousarray(v.reshape(v.shape[0], nchunk, 128).transpose(0, 2, 1))


def kernel(x, positions, norm_mix, w_in, b_gate, norm_q, w_uq, norm_kv, w_ukv,
           w_oa, w_ob, w_out, norm_ffn, w_up, conv_w, conv_b, w_down, norm_final):
    nvc = NCORES
    x = np.asarray(x, np.float32)
    pos = np.asarray(positions).reshape(S).astype(np.int32)
    cst, perm128, perm64 = _consts()
    xT = np.ascontiguousarray(x[0].reshape(nvc, TOK, DC, 128).transpose(0, 2, 3, 1))
    masks = np.zeros((nvc, 128, 4, 256), np.float32)
    for vc in range(nvc):
        masks[vc] = _dil_masks(vc, nvc)
    bg = np.asarray(b_gate, np.float32)
    bgate = np.ascontiguousarray(bg.reshape(DEPTH, 2, DC, 128).transpose(0, 3, 1, 2))
    cwv = np.asarray(conv_w, np.float32)
    cw = np.ascontiguousarray(cwv.reshape(DEPTH, 3, 2 * FC, 128).transpose(0, 3, 1, 2))
    in_map = {
        "xT": xT,
        "pos": np.ascontiguousarray(np.broadcast_to(pos.reshape(nvc, 1, TOK), (nvc, 64, TOK))).astype(np.int32),
        "masks": masks, "cst": cst, "perm128": perm128, "perm64": perm64,
        "gmix": _fm2(norm_mix, DC), "gffn": _fm2(norm_ffn, DC), "gfin": _fm(norm_final, DC),
        "bgate": bgate, "nq": _fm2(norm_q, 4), "nkv": _fm2(norm_kv, 4),
        "conv_w": cw, "conv_b": _fm2(conv_b, 2 * FC),
        "w_in": np.asarray(w_in, np.float32), "w_uq": np.asarray(w_uq, np.float32), "w_ukv": np.asarray(w_ukv, np.float32),
        "w_oa": np.asarray(w_oa, np.float32), "w_ob": np.asarray(w_ob, np.float32), "w_out": np.asarray(w_out, np.float32),
        "w_up": np.asarray(w_up, np.float32), "w_down": np.asarray(w_down, np.float32),
    }
    nc = build_fused(nvc, DEPTH)
    res = run_bass_kernel_spmd(nc, [in_map], core_ids=[0])
    o = np.asarray(res.results[0]["out"], np.float32)
    out = o.transpose(0, 3, 1, 2).reshape(1, S, D)
    return np.ascontiguousarray(out)
```
